# Optimizing a Trainium2 kernel written in Bass

```python
import math
import jax, jax.numpy as jnp
from jax import lax
import numpy as np

D_MODEL = 1024
BATCH = 4
SEQ = 4096
DEPTH = 4

CHUNK = 64
Q_BLOCK = 128
EPS = 1e-5
CONV_K = 4
D_MIX = D_MODEL
D_FF = 2816

DA_HEADS = 4
DA_QK = D_MODEL // 32
DA_V = 2 * DA_QK
DA_ROT = DA_QK // 4
ROPE_THETA = 500000.0
M_HEADS = 8
M_HEADDIM = 64
M_INNER = M_HEADS * M_HEADDIM
M_GROUPS = 2
M_STATE = 128
G_HEADS = 4
G_DK = 64
G_DV = 64

ALPHA = (2.0 * DEPTH) ** 0.25
BETA_INIT = (8.0 * DEPTH) ** -0.25

IN_SIZES = (DA_HEADS * 2 * DA_QK, DA_HEADS * 2 * DA_QK, DA_HEADS * DA_V,
            M_INNER, M_INNER + 2 * M_GROUPS * M_STATE, M_HEADS,
            G_HEADS * G_DK, G_HEADS * G_DK, G_HEADS * G_DV, G_HEADS * G_DV, G_HEADS, G_HEADS)
D_IN = sum(IN_SIZES)
IN_SPLITS = tuple(int(s) for s in np.cumsum(IN_SIZES)[:-1])

kernel_name = 'chunk_causal_hybrid_head_groups'


def layer_norm(x, g, b):
    xf = x.astype(jnp.float32)
    mu = jnp.mean(xf, -1, keepdims=True)
    var = jnp.mean(jnp.square(xf - mu), -1, keepdims=True)
    return ((xf - mu) * lax.rsqrt(var + EPS) * g + b).astype(x.dtype)


def rms_norm(x, w):
    xf = x.astype(jnp.float32)
    return xf * lax.rsqrt(jnp.mean(xf * xf, -1, keepdims=True) + EPS) * w


def l2_normalize(x):
    return x * lax.rsqrt(jnp.sum(x * x, -1, keepdims=True) + 1e-6)


def swiglu(x, w_gu, w_down):
    gate, up = jnp.split(x @ w_gu, 2, axis=-1)
    return (jax.nn.silu(gate) * up) @ w_down


def causal_dwconv(x, w):
    c = x.shape[-1]
    return lax.conv_general_dilated(
        x, w[:, None, :].astype(x.dtype), window_strides=(1,), padding=[(CONV_K - 1, 0)],
        dimension_numbers=('NWC', 'WIO', 'NWC'), feature_group_count=c)


def lambda_init(layer_idx):
    return 0.8 - 0.6 * math.exp(-0.3 * layer_idx)


def rope_tables(seq_len):
    pos = jnp.arange(seq_len, dtype=jnp.float32)
    inv_freq = ROPE_THETA ** (-jnp.arange(0, DA_ROT, 2, dtype=jnp.float32) / DA_ROT)
    ang = pos[:, None] * inv_freq[None, :]
    return jnp.cos(ang), jnp.sin(ang)


def apply_partial_rotary(t, cos, sin):
    half = DA_ROT // 2
    c = cos[None, :, None, None, :]
    s = sin[None, :, None, None, :]
    tf = t.astype(jnp.float32)
    t1, t2, rest = tf[..., :half], tf[..., half:DA_ROT], tf[..., DA_ROT:]
    return jnp.concatenate([t1 * c - t2 * s, t2 * c + t1 * s, rest], -1).astype(t.dtype)


def diff_attention_mixer(dq, dk, dv, cos, sin, lam_params, subln_w, lam_init):
    bsz, s_len, _ = dq.shape
    q = apply_partial_rotary(dq.reshape(bsz, s_len, DA_HEADS, 2, DA_QK), cos, sin)
    k = apply_partial_rotary(dk.reshape(bsz, s_len, DA_HEADS, 2, DA_QK), cos, sin)
    v = dv.reshape(bsz, s_len, DA_HEADS, DA_V)
    lp = lam_params.astype(jnp.float32)
    lam = jnp.exp(jnp.sum(lp[0] * lp[1])) - jnp.exp(jnp.sum(lp[2] * lp[3])) + lam_init
    nb = s_len // Q_BLOCK
    key_chunk = jnp.arange(s_len) // CHUNK
    scale = DA_QK ** -0.5
    q_blocks = q.reshape(bsz, nb, Q_BLOCK, DA_HEADS, 2, DA_QK).swapaxes(0, 1)

    def one_block(args):
        qb, b_idx = args
        sc = jnp.einsum('bqhmd,bkhmd->bhmqk', qb, k).astype(jnp.float32) * scale
        q_chunk = (b_idx * Q_BLOCK + jnp.arange(Q_BLOCK)) // CHUNK
        mask = key_chunk[None, :] <= q_chunk[:, None]
        p = jax.nn.softmax(jnp.where(mask, sc, -jnp.inf), axis=-1)
        a = p[:, :, 0] - lam * p[:, :, 1]
        return jnp.einsum('bhqk,bkhd->bqhd', a.astype(v.dtype), v)

    o = lax.map(one_block, (q_blocks, jnp.arange(nb)))
    o = o.swapaxes(0, 1).reshape(bsz, s_len, DA_HEADS, DA_V)
    o = rms_norm(o, subln_w) * (1.0 - lam_init)
    return o.reshape(bsz, s_len, DA_HEADS * DA_V).astype(dq.dtype)


def segsum_exp(a_cs):
    l = a_cs.shape[-1]
    tril = jnp.tril(jnp.ones((l, l), dtype=bool))
    diff = a_cs[..., :, None] - a_cs[..., None, :]
    return jnp.where(tril, jnp.exp(jnp.where(tril, diff, 0.0)), 0.0)


def ssd_chunked(xh, dt, a_head, bm, cm):
    bsz, s_len, h, p = xh.shape
    n = bm.shape[-1]
    nc = s_len // CHUNK
    xdt = (xh.astype(jnp.float32) * dt[..., None]).reshape(bsz, nc, CHUNK, h, p)
    bc = bm.astype(jnp.float32).reshape(bsz, nc, CHUNK, h, n)
    cc = cm.astype(jnp.float32).reshape(bsz, nc, CHUNK, h, n)
    a = (dt * a_head).reshape(bsz, nc, CHUNK, h).transpose(0, 3, 1, 2)
    a_cs = jnp.cumsum(a, -1)
    lmat = segsum_exp(a_cs)
    cb = jnp.einsum('bclhn,bcshn->bhcls', cc, bc)
    y_diag = jnp.einsum('bhcls,bcshp->bclhp', cb * lmat, xdt)
    decay_states = jnp.exp(a_cs[..., -1:] - a_cs)
    states = jnp.einsum('bclhn,bhcl,bclhp->bchpn', bc, decay_states, xdt)
    chunk_decay = jnp.exp(a_cs[..., -1])

    def step(hst, inp):
        st, dec = inp
        return hst * dec[..., None, None] + st, hst

    h0 = jnp.zeros((bsz, h, p, n), jnp.float32)
    _, prev = lax.scan(step, h0, (states.swapaxes(0, 1), chunk_decay.transpose(2, 0, 1)))
    prev = prev.swapaxes(0, 1)
    y_off = jnp.einsum('bclhn,bchpn,bhcl->bclhp', cc, prev, jnp.exp(a_cs))
    return (y_diag + y_off).reshape(bsz, s_len, h, p)


def mamba2_mixer(mz, mxbc, mdt, conv_w, conv_b, dt_bias, a_log, d_skip, norm_w):
    bsz, s_len, _ = mz.shape
    xbc = jax.nn.silu(causal_dwconv(mxbc, conv_w) + conv_b)
    xs, bm, cm = jnp.split(xbc, [M_INNER, M_INNER + M_GROUPS * M_STATE], axis=-1)
    xh = xs.reshape(bsz, s_len, M_HEADS, M_HEADDIM)
    rep = M_HEADS // M_GROUPS
    bm = jnp.repeat(bm.reshape(bsz, s_len, M_GROUPS, M_STATE), rep, axis=2)
    cm = jnp.repeat(cm.reshape(bsz, s_len, M_GROUPS, M_STATE), rep, axis=2)
    dt = jax.nn.softplus(mdt.astype(jnp.float32) + dt_bias)
    a_head = -jnp.exp(a_log.astype(jnp.float32))
    y = ssd_chunked(xh, dt, a_head, bm, cm) + d_skip[:, None] * xh.astype(jnp.float32)
    y = y.reshape(bsz, s_len, M_INNER) * jax.nn.silu(mz.astype(jnp.float32))
    y = rms_norm(y.reshape(bsz, s_len, M_GROUPS, M_INNER // M_GROUPS), 1.0)
    y = y.reshape(bsz, s_len, M_INNER) * norm_w
    return y.astype(mz.dtype)


def chunk_gated_delta(q, k, v, g, beta):
    bsz, s_len, h, _ = q.shape
    dv = v.shape[-1]
    nc = s_len // CHUNK

    def to_chunks(t):
        return t.reshape(bsz, nc, CHUNK, h, -1).transpose(0, 3, 1, 2, 4)

    qc, kc, vc = to_chunks(q), to_chunks(k), to_chunks(v)
    bc = beta.reshape(bsz, nc, CHUNK, h).transpose(0, 3, 1, 2)
    gc = jnp.cumsum(g.reshape(bsz, nc, CHUNK, h).transpose(0, 3, 1, 2), -1)
    tril = jnp.tril(jnp.ones((CHUNK, CHUNK), dtype=bool))
    strict = jnp.tril(jnp.ones((CHUNK, CHUNK), dtype=bool), -1)
    decay = segsum_exp(gc)
    kb = kc * bc[..., None]
    vb = vc * bc[..., None]
    lmat = jnp.where(strict, jnp.einsum('bhcld,bhcsd->bhcls', kb, kc) * decay, 0.0)
    imat = jnp.eye(CHUNK, dtype=jnp.float32) + lmat
    u = lax.linalg.triangular_solve(imat, vb, left_side=True, lower=True, unit_diagonal=True)
    w = lax.linalg.triangular_solve(imat, kb * jnp.exp(gc)[..., None], left_side=True,
                                    lower=True, unit_diagonal=True)
    attn_intra = jnp.where(tril, jnp.einsum('bhcld,bhcsd->bhcls', qc, kc) * decay, 0.0)
    q_dec = qc * jnp.exp(gc)[..., None]
    k_dec = kc * jnp.exp(gc[..., -1:] - gc)[..., None]
    chunk_decay = jnp.exp(gc[..., -1])

    def step(st, inp):
        u_i, w_i, qd_i, kd_i, a_i, dec_i = inp
        v_new = u_i - jnp.einsum('bhld,bhdv->bhlv', w_i, st)
        o_i = jnp.einsum('bhld,bhdv->bhlv', qd_i, st) + jnp.einsum('bhls,bhsv->bhlv', a_i, v_new)
        st = st * dec_i[..., None, None] + jnp.einsum('bhld,bhlv->bhdv', kd_i, v_new)
        return st, o_i

    xs = tuple(jnp.moveaxis(t, 2, 0) for t in (u, w, q_dec, k_dec, attn_intra, chunk_decay))
    s0 = jnp.zeros((bsz, h, q.shape[-1], dv), jnp.float32)
    _, o = lax.scan(step, s0, xs)
    return o.transpose(1, 0, 3, 2, 4).reshape(bsz, s_len, h, dv)


def gated_deltanet_mixer(gq, gk, gv, gz, gb, ga, conv_w, a_log, dt_bias, norm_w):
    bsz, s_len, _ = gq.shape
    qkv = jax.nn.silu(causal_dwconv(jnp.concatenate([gq, gk, gv], -1), conv_w))
    q, k, v = jnp.split(qkv.astype(jnp.float32), [G_HEADS * G_DK, 2 * G_HEADS * G_DK], axis=-1)
    q = l2_normalize(q.reshape(bsz, s_len, G_HEADS, G_DK)) * (G_DK ** -0.5)
    k = l2_normalize(k.reshape(bsz, s_len, G_HEADS, G_DK))
    v = v.reshape(bsz, s_len, G_HEADS, G_DV)
    beta = jax.nn.sigmoid(gb.astype(jnp.float32))
    g = -jnp.exp(a_log.astype(jnp.float32)) * jax.nn.softplus(ga.astype(jnp.float32) + dt_bias)
    o = chunk_gated_delta(q, k, v, g, beta)
    o = rms_norm(o, norm_w) * jax.nn.silu(gz.astype(jnp.float32).reshape(bsz, s_len, G_HEADS, G_DV))
    return o.reshape(bsz, s_len, G_HEADS * G_DV).astype(gq.dtype)


def setup_inputs(seed: int = 0) -> dict:
    key = jax.random.key(seed)
    ks = iter(jax.random.split(key, 40))
    f32 = jnp.float32
    L = DEPTH

    def nrm(shape, scale):
        return jax.random.normal(next(ks), shape, f32) * scale

    def gain(shape):
        return 1.0 + nrm(shape, 0.02)

    def dt_bias_init(shape):
        u = jax.random.uniform(next(ks), shape, f32, minval=math.log(1e-3), maxval=math.log(1e-1))
        dt = jnp.exp(u)
        return dt + jnp.log(-jnp.expm1(-dt))

    def a_log_init(shape):
        return jnp.log(jax.random.uniform(next(ks), shape, f32, minval=1.0, maxval=16.0))

    return {
        'x': jax.random.normal(next(ks), (BATCH, SEQ, D_MODEL), f32),
        'ffn1_w_gu': nrm((L, D_MODEL, 2 * D_FF), D_MODEL ** -0.5),
        'ffn1_w_down': nrm((L, D_FF, D_MODEL), D_FF ** -0.5 * BETA_INIT),
        'ln1_g': gain((L, D_MODEL)),
        'ln1_b': nrm((L, D_MODEL), 0.02),
        'w_in': nrm((L, D_MODEL, D_IN), D_MODEL ** -0.5),
        'da_lambda': nrm((L, 4, DA_QK), 0.1),
        'da_subln_w': gain((L, DA_V)),
        'm_conv_w': nrm((L, CONV_K, M_INNER + 2 * M_GROUPS * M_STATE), CONV_K ** -0.5),
        'm_conv_b': nrm((L, M_INNER + 2 * M_GROUPS * M_STATE), 0.02),
        'm_dt_bias': dt_bias_init((L, M_HEADS)),
        'm_A_log': a_log_init((L, M_HEADS)),
        'm_D': gain((L, M_HEADS)),
        'm_norm_w': gain((L, M_INNER)),
        'g_conv_w': nrm((L, CONV_K, 2 * G_HEADS * G_DK + G_HEADS * G_DV), CONV_K ** -0.5),
        'g_A_log': a_log_init((L, G_HEADS)),
        'g_dt_bias': dt_bias_init((L, G_HEADS)),
        'g_norm_w': gain((L, G_DV)),
        'w_out': nrm((L, D_MIX, D_MODEL), D_MIX ** -0.5 * BETA_INIT),
        'ln2_g': gain((L, D_MODEL)),
        'ln2_b': nrm((L, D_MODEL), 0.02),
        'ffn2_w_gu': nrm((L, D_MODEL, 2 * D_FF), D_MODEL ** -0.5),
        'ffn2_w_down': nrm((L, D_FF, D_MODEL), D_FF ** -0.5 * BETA_INIT),
        'ln3_g': gain((L, D_MODEL)),
        'ln3_b': nrm((L, D_MODEL), 0.02),
    }


def reference(x, ffn1_w_gu, ffn1_w_down, ln1_g, ln1_b, w_in, da_lambda, da_subln_w,
              m_conv_w, m_conv_b, m_dt_bias, m_A_log, m_D, m_norm_w,
              g_conv_w, g_A_log, g_dt_bias, g_norm_w, w_out, ln2_g, ln2_b,
              ffn2_w_gu, ffn2_w_down, ln3_g, ln3_b):
    cos, sin = rope_tables(x.shape[1])
    for l in range(DEPTH):
        x = layer_norm(ALPHA * x + 0.5 * swiglu(x, ffn1_w_gu[l], ffn1_w_down[l]), ln1_g[l], ln1_b[l])
        h = x @ w_in[l]
        dq, dk, dv, mz, mxbc, mdt, gq, gk, gv, gz, gb, ga = jnp.split(h, IN_SPLITS, axis=-1)
        a_out = diff_attention_mixer(dq, dk, dv, cos, sin, da_lambda[l], da_subln_w[l], lambda_init(l))
        m_out = mamba2_mixer(mz, mxbc, mdt, m_conv_w[l], m_conv_b[l], m_dt_bias[l], m_A_log[l],
                             m_D[l], m_norm_w[l])
        g_out = gated_deltanet_mixer(gq, gk, gv, gz, gb, ga, g_conv_w[l], g_A_log[l], g_dt_bias[l],
                                     g_norm_w[l])
        mix = jnp.concatenate([a_out, m_out, g_out], -1).astype(x.dtype) @ w_out[l]
        x = layer_norm(ALPHA * x + mix, ln2_g[l], ln2_b[l])
        x = layer_norm(ALPHA * x + 0.5 * swiglu(x, ffn2_w_gu[l], ffn2_w_down[l]), ln3_g[l], ln3_b[l])
    return x
```

```python
import math
import numpy as np
import ml_dtypes
import concourse.bass as bass
import concourse.mybir as mybir
from concourse.bass_utils import run_bass_kernel_spmd


F32 = mybir.dt.float32
BF16 = mybir.dt.bfloat16
ALU = mybir.AluOpType
AF = mybir.ActivationFunctionType
AX = mybir.AxisListType


class Tile:
    __slots__ = ("name", "t", "last_w", "readers", "sem", "dma_cnt", "psum")

    def __init__(self, name, t):
        self.name = name
        self.t = t
        self.last_w = None
        self.readers = []
        self.sem = None
        self.dma_cnt = 0
        self.psum = False

    def __getitem__(self, idx):
        return self.t[idx]


class CALL:
    __slots__ = ("name", "args", "kw")

    def __init__(self, name, *args, **kw):
        self.name = name
        self.args = args
        self.kw = kw

    def __call__(self, eng):
        return getattr(eng, self.name)(*self.args, **self.kw)


class Op:
    __slots__ = ("eng", "fn", "deps", "signal", "sigval", "is_dma", "dma_tile", "dma_waits", "idx", "line")


ENGS = ("pe", "act", "dve", "pool", "sp")


class Sched:
    def __init__(self, nc):
        self.nc = nc
        self.ops = []
        self.last_op = {e: None for e in ENGS}
        self.dma_tiles = []
        self.debug = False
        self.log = []

    def sb(self, name, shape, dtype):
        return Tile(name, self.nc.alloc_sbuf_tensor("s_" + name, list(shape), dtype))

    def make_arena(self, nbytes):
        self.arena = self.nc.alloc_sbuf_tensor("s_arena", [128, nbytes // 4], F32)
        self.arena_bytes = nbytes
        self.aptr = 0

    def arena_reset(self):
        self.aptr = 0

    def av(self, name, shape, dtype):
        dsize = 4 if dtype == F32 else 2
        nelem = 1
        for d in shape[1:]:
            nelem *= d
        nbytes = (nelem * dsize + 31) // 32 * 32
        off = self.aptr
        assert off + nbytes <= self.arena_bytes, f"arena overflow at {name}: {off + nbytes} > {self.arena_bytes}"
        self.aptr += nbytes
        ap = self.arena[0:shape[0], off // 4:(off + nbytes) // 4]
        if dtype != F32:
            ap = ap.bitcast(dtype)
        ap = ap[:, 0:nelem]
        if len(shape) == 3:
            ap = ap.rearrange("p (a b) -> p a b", a=shape[1])
        elif len(shape) == 4:
            ap = ap.rearrange("p (a b c) -> p a b c", a=shape[1], b=shape[2])
        return Tile(name, ap)

    def ps(self, name, shape, dtype=F32):
        t = Tile(name, self.nc.alloc_psum_tensor("p_" + name, list(shape), dtype))
        t.psum = True
        return t

    def add(self, eng, fn, reads=(), writes=(), dma_dst=None):
        op = Op()
        op.eng = eng
        op.fn = fn
        op.signal = False
        op.sigval = None
        op.is_dma = dma_dst is not None
        op.dma_tile = dma_dst
        op.idx = len(self.ops)
        if self.debug:
            import sys as _sys
            f = _sys._getframe(1)
            if f.f_code.co_name == "dma":
                f = f.f_back
            op.line = f"{f.f_code.co_name}:{f.f_lineno}"
        else:
            op.line = ""
        deps = {}
        for r in reads:
            if r.last_w is not None:
                deps[id(r.last_w)] = r.last_w
            if r.psum:
                for rd in r.readers:
                    if rd.eng != eng:
                        deps[id(rd)] = rd
        for w in writes:
            if w.last_w is not None:
                deps[id(w.last_w)] = w.last_w
            for rd in w.readers:
                deps[id(rd)] = rd
        deps.pop(id(op), None)
        dl = []
        dma_waits = {}
        for d in deps.values():
            if d.is_dma:
                t = d.dma_tile
                dma_waits[id(t)] = (t, t.dma_cnt)
            elif d.eng == "pe" and eng == "pe" and not op.is_dma:
                continue
            else:
                dl.append(d)
        op.deps = dl
        op.dma_waits = list(dma_waits.values())
        for r in reads:
            r.readers.append(op)
        for w in writes:
            w.last_w = op
            w.readers = []
        if op.is_dma:
            t = dma_dst
            if t.sem is None:
                t.sem = self.nc.alloc_semaphore("d_" + t.name)
                self.dma_tiles.append(t)
            t.dma_cnt += 16
        self.ops.append(op)
        self.last_op[eng] = op
        return op

    def dma(self, q, dst_tile, out_ap, in_ap, reads=(), extra_writes=(), **kw):
        return self.add(q, lambda e: e.dma_start(out=out_ap, in_=in_ap, **kw),
                        reads=reads, writes=(dst_tile,) + tuple(extra_writes), dma_dst=dst_tile)

    def barrier(self):
        lasts = [o for o in self.last_op.values() if o is not None]
        dts = [(t, t.dma_cnt) for t in self.dma_tiles]
        for e in ENGS:
            op = Op()
            op.eng = e
            op.fn = None
            op.signal = False
            op.sigval = None
            op.is_dma = False
            op.dma_tile = None
            op.idx = len(self.ops)
            op.line = "barrier"
            op.deps = [o for o in lasts if not o.is_dma]
            op.dma_waits = list(dts)
            self.ops.append(op)

    def emit(self):
        nc = self.nc
        for op in self.ops:
            for d in op.deps:
                d.signal = True
        cnt = {e: 0 for e in ENGS}
        for op in self.ops:
            if op.signal and not op.is_dma:
                cnt[op.eng] += 1
                op.sigval = cnt[op.eng]
        esem = {e: nc.alloc_semaphore("e_" + e) for e in ENGS}
        per = {e: [o for o in self.ops if o.eng == e] for e in ENGS}
        stats = {"waits": 0, "insts": 0}
        semname = {id(esem[e]): e for e in ENGS}
        for t in self.dma_tiles:
            semname[id(t.sem)] = "d:" + t.name

        def run(ename, eng):
            waited = {}
            for op in per[ename]:
                ws = {}
                for d in op.deps:
                    s = esem[d.eng]
                    ws[id(s)] = (s, max(ws.get(id(s), (None, 0))[1], d.sigval))
                for (t, v) in op.dma_waits:
                    ws[id(t.sem)] = (t.sem, max(ws.get(id(t.sem), (None, 0))[1], v))
                wl = []
                for (s, v) in ws.values():
                    if waited.get(id(s), 0) < v:
                        eng.wait_ge(s, v)
                        waited[id(s)] = v
                        stats["waits"] += 1
                        wl.append((semname.get(id(s), "?"), v))
                if self.debug:
                    sig = ""
                    if op.is_dma:
                        sig = f"-> {op.dma_tile.name}+16"
                    elif op.signal:
                        sig = f"-> {ename}={op.sigval}"
                    self.log.append(f"{ename:5s} #{op.idx:6d} {op.line:28s} waits={wl} {sig}")
                if op.fn is None:
                    continue
                inst = op.fn(eng)
                stats["insts"] += 1
                if op.is_dma:
                    inst.then_inc(op.dma_tile.sem, 16)
                elif op.signal:
                    inst.then_inc(esem[ename], 1)

        with nc.Block() as block:
            @block.tensor
            def _(e):
                run("pe", e)

            @block.scalar
            def _(e):
                run("act", e)

            @block.vector
            def _(e):
                run("dve", e)

            @block.gpsimd
            def _(e):
                run("pool", e)

            @block.sync
            def _(e):
                run("sp", e)
        return stats


D = 1024
EPS = 1e-5
ALPHA = 8.0 ** 0.25

def rowblocks():
    rb = []
    for side in ("q", "k"):
        for c in range(2):
            rb.append((f"a{side}m{c}", 128))
            rb.append((f"a{side}p{c}", 128))
    for h in range(8):
        rb.append((f"mz{h}", 64))
    for h in range(8):
        rb.append((f"mx{h}", 64))
    for nm in ("gq", "gk", "gv", "gz"):
        for h in range(4):
            rb.append((f"{nm}{h}", 64))
    for g in range(2):
        rb.append((f"mB{g}", 128))
    for g in range(2):
        rb.append((f"mC{g}", 128))
    return rb


RB = rowblocks()
RB_OFF = {}
_o = 0
for _n, _m in RB:
    RB_OFF[_n] = (_o, _m)
    _o += _m
NROWS = _o

IN_SIZES = (256, 256, 256, 512, 1024, 8, 256, 256, 256, 256, 4, 4)
IN_OFF = np.concatenate([[0], np.cumsum(IN_SIZES)])


def fm_cols():
    cols = []
    perm = np.arange(32)
    perm[0:4] = np.arange(4, 8)
    perm[4:8] = np.arange(0, 4)
    for si, side in enumerate(("q", "k")):
        base = IN_OFF[si]
        for c in range(2):
            cols.append(np.concatenate([base + hm * 32 + np.arange(32) for hm in range(4 * c, 4 * c + 4)]))
            cols.append(np.concatenate([base + hm * 32 + perm for hm in range(4 * c, 4 * c + 4)]))
    cols.append(IN_OFF[3] + np.arange(512))
    cols.append(IN_OFF[4] + np.arange(512))
    for j in (6, 7, 8, 9):
        cols.append(IN_OFF[j] + np.arange(256))
    cols.append(IN_OFF[4] + 512 + np.arange(256))
    cols.append(IN_OFF[4] + 768 + np.arange(256))
    return np.concatenate(cols)


FM_COLS = fm_cols()
TM_COLS = np.concatenate([IN_OFF[2] + np.arange(256), IN_OFF[5] + np.arange(8), IN_OFF[10] + np.arange(4), IN_OFF[11] + np.arange(4)])
NTM = 272


def rope_consts(T):
    pos = np.arange(T, dtype=np.float32)
    inv = (500000.0 ** (-np.arange(0, 8, 2, dtype=np.float32) / 8.0)).astype(np.float32)
    ang = pos[:, None] * inv[None, :]
    c, s = np.cos(ang).astype(np.float32), np.sin(ang).astype(np.float32)
    CT = np.ones((32, T), np.float32)
    ST = np.zeros((32, T), np.float32)
    CT[0:4] = c.T
    CT[4:8] = c.T
    ST[0:4] = -s.T
    ST[4:8] = s.T
    return np.ascontiguousarray(np.tile(CT, (4, 1))), np.ascontiguousarray(np.tile(ST, (4, 1)))


def attn_mask():
    m = np.zeros((128, 4, 512), np.float32)
    for r in range(4):
        k = r * 128 + np.arange(128)
        q = np.arange(512)
        m[:, r, :] = (k[:, None] // 64) <= (q[None, :] // 64)
    return m.astype(ml_dtypes.bfloat16)


def pack_mixer(P):
    w_in = P["w_in"]
    wfm = w_in[:, FM_COLS]
    wfm_l = np.ascontiguousarray(wfm.reshape(8, 128, NROWS // 512, 512).transpose(2, 1, 0, 3))
    wtm = w_in[:, TM_COLS]
    wtm_l = np.ascontiguousarray(wtm.reshape(8, 128, NTM).transpose(1, 0, 2))
    out = dict(wfm=wfm_l, wtm=wtm_l)
    out["da_lambda"] = np.ascontiguousarray(P["da_lambda"].reshape(1, 128))
    out["subw"] = np.ascontiguousarray(P["da_subln_w"].reshape(64, 1))
    return out


def phase_B1(S, c, x1T, hT, v_tm, sm_tm, wfm_l, wtm_l, T, d_x1=None):
    xbf = c["b1_xbf"]
    wbuf = c["b1_w"]
    stg = c["b1_stg"]
    ps = c["ps"]
    x1v = x1T.rearrange("(k p) t -> p k t", p=128)
    for k in range(8):
        for t0 in range(0, T, 2048):
            t1 = min(T, t0 + 2048)
            S.dma("pool", xbf, xbf[:, k, t0:t1], x1v[:, k, t0:t1])
    ntile = T // 512
    it = 0
    sgi = 0
    for wb in range(NROWS // 512):
        wt = wbuf[wb % 2]
        S.dma("sp", wt, wt[:, :, :], wfm_l[wb, :, :, :], reads=(c["d_wbf"],))
        for col in range(0, 512, 128):
            r0 = wb * 512 + col
            for tg in range(0, ntile, 4):
                sg = stg[sgi % 3]
                sgi += 1
                nt_ = min(4, ntile - tg)
                for tt in range(nt_):
                    t = tg + tt
                    pt = ps[it % 6]
                    it += 1
                    for k in range(8):
                        S.add("pe", CALL("matmul", pt[:, :], wt[:, k, col:col + 128], xbf[:, k, t * 512:(t + 1) * 512],
                                         start=(k == 0), stop=(k == 7)), reads=(wt, xbf), writes=(pt,))
                    if it % 2 == 0:
                        S.add("act", CALL("activation", sg[:, tt * 512:(tt + 1) * 512], pt[:, :], AF.Identity), reads=(pt,), writes=(sg,))
                    else:
                        S.add("dve", CALL("tensor_copy", sg[:, tt * 512:(tt + 1) * 512], pt[:, :]), reads=(pt,), writes=(sg,))
                S.dma("sp", c["d_hT"], hT[r0:r0 + 128, tg * 512:(tg + nt_) * 512], sg[:, 0:nt_ * 512], reads=(sg,))
    wtm = c["b1_wtm"]
    S.dma("sp", wtm, wtm[:, :, :], wtm_l[:, :, :], reads=(c["d_wbf"],))
    for n in range(T // 128):
        pt = ps[n % 6]
        sg = stg[n % 3]
        for k in range(8):
            S.add("pe", CALL("matmul",
                pt[:, 0:NTM], xbf[:, k, n * 128:(n + 1) * 128], wtm[:, k, :], start=(k == 0), stop=(k == 7)),
                reads=(wtm, xbf), writes=(pt,))
        S.add("act", CALL("activation", sg[:, 0:NTM], pt[:, 0:NTM], AF.Identity),
              reads=(pt,), writes=(sg,))
        S.dma("sp", c["d_vtm"], v_tm[n * 128:(n + 1) * 128, :], sg[:, 0:256], reads=(sg,))
        S.dma("sp", c["d_sm"], sm_tm[n * 128:(n + 1) * 128, :], sg[:, 256:272], reads=(sg,))


def phase_A(S, c, hT, v_tm, catT, CT_d, ST_d, mask_d, lam_d, subw_d, lam_init, T):
    ps = c["ps"]
    Kc = c["a_Kc"]
    Va = c["a_V"]
    Qall = c["a_Qall"]
    Qp = c["a_Qp"]
    ld = c["a_ld"]
    cs = c["a_cs"]
    tmp = c["a_tmp"]
    pb = c["a_P"]
    mask = c["a_mask"]
    ones32 = c["ones32"]
    lamt = c["a_lam"]
    sb64 = c["a_sb64"]
    aob = c["a_aob"]
    subw = c["a_subw"]
    xo = c["a_xo"]
    sel = c["a_sel"]
    epsT = c["epsT"]
    gc = c["gconst"]
    d_hT, d_vtm, d_cat = c["d_hT"], c["d_vtm"], c["d_cat"]
    scale = 32.0 ** -0.5
    ntile = T // 512
    sbanks = [ps[0], ps[1], c["psTf"]]

    S.dma("sp", mask, mask[:, :, :], mask_d)
    S.add("dve", CALL("memset", sel[:, :], 0.0), writes=(sel,))
    S.add("dve", CALL("memset", sel[64:65, :], 1.0), writes=(sel,))
    S.dma("sp", subw, subw[:, :], subw_d)
    S.dma("sp", lamt, lamt[:, 0:128], lam_d.partition_broadcast(128))
    for j in range(2):
        S.add("dve", CALL("tensor_tensor", lamt[:, 0:32] if j == 0 else lamt[:, 64:96],
                          lamt[:, 64 * j:64 * j + 32], lamt[:, 64 * j + 32:64 * j + 64], ALU.mult), reads=(lamt,), writes=(lamt,))
        S.add("dve", CALL("tensor_reduce", lamt[:, 130 + j:131 + j], lamt[:, 64 * j:64 * j + 32], AX.X, ALU.add), reads=(lamt,), writes=(lamt,))
    S.add("act", CALL("activation", lamt[:, 132:134], lamt[:, 130:132], AF.Exp), reads=(lamt,), writes=(lamt,))
    S.add("dve", CALL("tensor_tensor", lamt[:, 134:135], lamt[:, 133:134], lamt[:, 132:133], ALU.subtract), reads=(lamt,), writes=(lamt,))
    S.add("dve", CALL("tensor_scalar", lamt[:, 136:137], lamt[:, 134:135], -float(lam_init), 0.0, ALU.add, ALU.add), reads=(lamt,), writes=(lamt,))
    neglam = lamt[0:64, 136:137]

    ri = [0]

    def rotary(dst_ap, dst_tile, name_main, name_part, col0):
        i = ri[0]
        ri[0] += 1
        l0, l1 = ld[(2 * i) % 4], ld[(2 * i + 1) % 4]
        cst = cs[i % 2]
        t0, t1 = tmp[0], tmp[1]
        r0 = RB_OFF[name_main][0]
        r1 = RB_OFF[name_part][0]
        S.dma("sp", l0, l0[:, :], hT[r0:r0 + 128, col0:col0 + 512], reads=(d_hT,))
        S.dma("sp", l1, l1[:, :], hT[r1:r1 + 128, col0:col0 + 512], reads=(d_hT,))
        S.dma("sp", cst, cst[:, 0, :], CT_d[:, col0:col0 + 512])
        S.dma("sp", cst, cst[:, 1, :], ST_d[:, col0:col0 + 512])
        S.add("dve", CALL("tensor_tensor", t0[:, :], l0[:, :], cst[:, 0, :], ALU.mult), reads=(l0, cst), writes=(t0,))
        S.add("pool", CALL("tensor_tensor", t1[:, :], l1[:, :], cst[:, 1, :], ALU.mult), reads=(l1, cst), writes=(t1,))
        S.add("dve", CALL("tensor_tensor", dst_ap, t0[:, :], t1[:, :], ALU.add), reads=(t0, t1), writes=(dst_tile,))

    for cch in range(2):
        for kb in range(ntile):
            rotary(Kc[:, cch, kb * 512:(kb + 1) * 512], Kc, f"akm{cch}", f"akp{cch}", kb * 512)
    S.add("dve", CALL("memset", Va[:, :, 64:128], 1.0), writes=(Va,))
    pi = 0
    si = 0
    for h in range(4):
        cch = h // 2
        if h % 2 == 0:
            for t in range(ntile):
                rotary(Qall[:, t, :], Qall, f"aqm{cch}", f"aqp{cch}", t * 512)
        S.dma("pool", Va, Va[:, :, 0:64], v_tm[:, h * 64:(h + 1) * 64].rearrange("(n p) d -> p n d", p=128), reads=(d_vtm,))
        def prep_q(t):
            for m_ in range(2):
                j = (h % 2) * 2 + m_
                qp = Qp[(t % 2) * 2 + m_]
                S.add("dve", CALL("tensor_scalar", qp[:, :], Qall[:, t, :], gc[:, GC["cmask"] + j:GC["cmask"] + j + 1], 0.0, ALU.mult, ALU.add),
                      reads=(Qall, gc), writes=(qp,))

        prep_q(0)
        for t in range(ntile):
            Qpt = [Qp[(t % 2) * 2], Qp[(t % 2) * 2 + 1]]
            nk = 4 * (t + 1)
            units = [(m_, kt) for m_ in range(2) for kt in range(nk)]
            ubuf = {}

            def emit_S(u):
                nonlocal si, pi
                m_, kt = u
                sp_ = sbanks[si % 3]
                si += 1
                P = pb[pi % 4]
                pi += 1
                ubuf[u] = (sp_, P)
                S.add("pe", CALL("matmul", sp_[:, :], Kc[:, cch, kt * 128:(kt + 1) * 128], Qpt[m_][:, :], start=True, stop=True),
                      reads=(Kc, Qpt[m_]), writes=(sp_,))

            def emit_rest(u):
                m_, kt = u
                sp_, P = ubuf.pop(u)
                po = ps[2 + m_]
                S.add("act", CALL("activation", P[:, :], sp_[:, :], AF.Exp, scale=scale), reads=(sp_,), writes=(P,))
                if kt >= 4 * t:
                    r = kt - 4 * t
                    S.add("dve", CALL("tensor_tensor", P[:, :], P[:, :], mask[:, r, :], ALU.mult), reads=(P, mask), writes=(P,))
                S.add("pe", CALL("matmul", po[:, :], Va[:, kt, :], P[:, :], start=(kt == 0), stop=(kt == nk - 1)),
                      reads=(Va, P), writes=(po,))

            SK = 2
            for u in units[0:SK]:
                emit_S(u)
            for i, u in enumerate(units):
                if i + SK < len(units):
                    emit_S(units[i + SK])
                emit_rest(u)
            if t + 1 < ntile:
                prep_q(t + 1)
            rl0, on0, rl1, on1, a, asq = sb64
            for m_, (rl_, on_) in enumerate(((rl0, on0), (rl1, on1))):
                x_ = xo[m_]
                S.add("act", CALL("activation", x_[:, :], ps[2 + m_][:, :], AF.Identity), reads=(ps[2 + m_],), writes=(x_,))
                S.add("pe", CALL("matmul", ps[4 + m_][0:64, :], sel[:, :], x_[:, :], start=True, stop=True),
                      reads=(sel, x_), writes=(ps[4 + m_],))
                S.add("dve", CALL("reciprocal", rl_[:, :], ps[4 + m_][0:64, :]), reads=(ps[4 + m_],), writes=(rl_,))
                S.add("dve", CALL("tensor_tensor", on_[:, :], x_[0:64, :], rl_[:, :], ALU.mult), reads=(x_, rl_), writes=(on_,))
            S.add("dve", CALL("scalar_tensor_tensor", a[:, :], on1[:, :], neglam, on0[:, :], ALU.mult, ALU.add),
                  reads=(on1, on0, lamt), writes=(a,))
            S.add("act", CALL("activation", asq[:, :], a[:, :], AF.Square), reads=(a,), writes=(asq,))
            S.add("pe", CALL("matmul", ps[6][0:64, :], ones32[0:64, 0:64], asq[:, :], start=True, stop=True),
                  reads=(ones32, asq), writes=(ps[6],))
            S.add("act", CALL("activation", asq[:, :], ps[6][0:64, :], AF.Ln, bias=epsT[0:64, 1:2], scale=1.0 / 64),
                  reads=(ps[6], epsT), writes=(asq,))
            S.add("act", CALL("activation", asq[:, :], asq[:, :], AF.Exp, scale=-0.5), reads=(asq,), writes=(asq,))
            S.add("dve", CALL("tensor_tensor", a[:, :], a[:, :], asq[:, :], ALU.mult), reads=(a, asq), writes=(a,))
            ao = aob[t % 2]
            S.add("dve", CALL("tensor_scalar", ao[:, :], a[:, :], subw[:, 0:1], 1.0 - float(lam_init), ALU.mult, ALU.mult),
                  reads=(a, subw), writes=(ao,))
            S.dma("sp", d_cat, catT[h * 64:(h + 1) * 64, t * 512:(t + 1) * 512], ao[:, :], reads=(ao,))


def pack_mamba(P):
    cw = P["m_conv_w"]
    cb = P["m_conv_b"]
    out = {}
    out["m_cwx"] = np.ascontiguousarray(cw[:, 0:512].reshape(4, 8, 64).transpose(2, 1, 0))
    out["m_cwbc"] = np.ascontiguousarray(cw[:, 512:1024].reshape(4, 4, 128).transpose(2, 1, 0))
    out["m_cbx"] = np.ascontiguousarray(cb[0:512].reshape(8, 64).T)
    out["m_cbbc"] = np.ascontiguousarray(cb[512:1024].reshape(4, 128).T)
    out["m_dtb"] = np.ascontiguousarray(P["m_dt_bias"].reshape(1, 8))
    out["m_alog"] = np.ascontiguousarray(P["m_A_log"].reshape(1, 8))
    out["m_D"] = np.ascontiguousarray(np.repeat(P["m_D"].reshape(1, 8), 64, 0))
    out["m_nw"] = np.ascontiguousarray(P["m_norm_w"].reshape(8, 64).T)
    return out


class StopHere(Exception):
    pass


def dbg(S, c, name, tile, ap):
    f = c.get("dbgfn")
    if f is not None:
        f(S, name, tile, ap)


def chk(c, tag):
    if c.get("stop") == tag:
        raise StopHere(tag)


def tri_const():
    return np.triu(np.ones((128, 128), np.float32))


def conv_block(S, w_tile, x, xt, w_ap_fn, acc, acct, M, out_ap, out_tile, bias_ap, bias_tile):
    S.add("dve", CALL("tensor_scalar", acc[0:M, :], x[0:M, 0:512], w_ap_fn(0), 0.0, ALU.mult, ALU.add),
          reads=(xt, w_tile), writes=(acct,))
    for j in range(1, 4):
        S.add("dve", CALL("scalar_tensor_tensor", acc[0:M, :], x[0:M, j:j + 512], w_ap_fn(j), acc[0:M, :], ALU.mult, ALU.add),
              reads=(xt, acct, w_tile), writes=(acct,))
    if bias_ap is not None:
        S.add("act", CALL("activation", out_ap, acc[0:M, :], AF.Silu, bias=bias_ap), reads=(acct, bias_tile), writes=(out_tile,))
    else:
        S.add("act", CALL("activation", out_ap, acc[0:M, :], AF.Silu), reads=(acct,), writes=(out_tile,))


def load_halo(S, dst, hT, r0, M, t, d_hT):
    if t == 0:
        S.add("pool", CALL("memset", dst[0:M, 0:3], 0.0), writes=(dst,))
        S.dma("sp", dst, dst[0:M, 3:515], hT[r0:r0 + M, 0:512], reads=(d_hT,))
    else:
        S.dma("sp", dst, dst[0:M, 0:515], hT[r0:r0 + M, t * 512 - 3:t * 512 + 512], reads=(d_hT,))


def phase_M(S, c, hT, sm_tm, catT, W, T):
    ps = c["ps"]
    d_hT, d_sm, d_cat = c["d_hT"], c["d_sm"], c["d_cat"]
    NB = T // 128
    ntile = T // 512
    ones32 = c["ones32"]
    epsT = c["epsT"]
    tri = c["tri"]
    identb = c["identb"]
    cwx, cwbc, cbx, cbbc = c["m_cwx"], c["m_cwbc"], c["m_cbx"], c["m_cbbc"]
    Dt, nw = c["m_Dt"], c["m_nw"]
    sm128 = c["sm128"]
    dt, aa, acs = c["m_dt"], c["m_a"], c["m_acs"]
    bcast = c["m_bc"]
    xin = c["m_xin"]
    acc = c["m_acc"]
    xsT = c["m_xsT"]
    xsb = c["m_xsb"]
    BT, CTt = c["m_BT"], c["m_CT"]
    xs_tok = c["m_xs_tok"]
    B_tok = c["m_B_tok"]
    rhsR = c["m_rhsR"]
    ER = c["m_ER"]
    Eh = c["m_E"]
    Gm = c["m_Gm"]
    Mp = c["m_Mp"]
    Cdec = c["m_Cdec"]
    xdt = c["m_xdt"]
    xsw = c["m_xsw"]
    wv = c["m_wv"]
    H = c["m_H"]
    Hbf = c["m_Hbf"]
    ysb = c["m_ysb"]
    mzs = c["m_mz"]
    ysq = c["m_ysq"]
    rs = c["m_rs"]
    yo = c["m_yo"]

    S.dma("sp", cwx, cwx[:, :, :], W["m_cwx"])
    S.dma("sp", cwbc, cwbc[:, :, :], W["m_cwbc"])
    S.dma("sp", cbx, cbx[:, :], W["m_cbx"])
    S.dma("sp", cbbc, cbbc[:, :], W["m_cbbc"])
    S.dma("sp", Dt, Dt[:, :], W["m_D"])
    S.dma("sp", nw, nw[:, :], W["m_nw"])
    S.dma("sp", bcast, bcast[:, 0:8], W["m_dtb"].partition_broadcast(128))
    S.dma("sp", bcast, bcast[:, 8:16], W["m_alog"].partition_broadcast(128))
    S.dma("sp", sm128, sm128[:, :, :], sm_tm.rearrange("(n p) c -> p n c", p=128), reads=(d_sm,))
    S.add("act", CALL("activation", bcast[:, 8:16], bcast[:, 8:16], AF.Exp), reads=(bcast,), writes=(bcast,))
    S.add("dve", CALL("tensor_scalar", bcast[:, 8:16], bcast[:, 8:16], -1.0, 0.0, ALU.mult, ALU.add), reads=(bcast,), writes=(bcast,))
    S.add("dve", CALL("tensor_tensor", dt[:, :, :], sm128[:, :, 0:8], bcast[:, 0:8].unsqueeze(1).to_broadcast([128, NB, 8]), ALU.add),
          reads=(sm128, bcast), writes=(dt,))
    S.add("act", CALL("activation", dt[:, :, :], dt[:, :, :], AF.Exp), reads=(dt,), writes=(dt,))
    S.add("act", CALL("activation", dt[:, :, :], dt[:, :, :], AF.Ln, bias=1.0), reads=(dt,), writes=(dt,))
    S.add("dve", CALL("tensor_tensor", aa[:, :, :], dt[:, :, :], bcast[:, 8:16].unsqueeze(1).to_broadcast([128, NB, 8]), ALU.mult),
          reads=(dt, bcast), writes=(aa,))
    S.add("pe", CALL("matmul", ps[0][:, 0:NB * 8], tri[:, :], aa[:, :, :].rearrange("p n h -> p (n h)"), start=True, stop=True),
          reads=(tri, aa), writes=(ps[0],))
    S.add("dve", CALL("tensor_copy", acs[:, :, :].rearrange("p n h -> p (n h)"), ps[0][:, 0:NB * 8]), reads=(ps[0],), writes=(acs,))
    S.add("pool", CALL("memset", H[:, :, :], 0.0), writes=(H,))
    S.add("pool", CALL("memset", Hbf[:, :, :], 0.0), writes=(Hbf,))

    chk(c, "smalls")
    psR = [ps[0], ps[1]]
    psG, psY0, psY1, psH, psS = ps[2], ps[3], ps[4], ps[5], ps[6]
    psTb = c["psTb"]

    def Rap(h):
        return psR[h // 4][:, (h % 4) * 128:(h % 4 + 1) * 128]

    for t in range(ntile):
        for h in range(8):
            xi, ac = xin[h % 2], acc[h % 2]
            load_halo(S, xi, hT, RB_OFF[f"mx{h}"][0], 64, t, d_hT)
            conv_block(S, cwx, xi, xi, lambda j, h=h: cwx[:, h, j:j + 1], ac, ac, 64, xsT[:, h, :], xsT, cbx[:, h:h + 1], cbx)
        S.add("pool", CALL("tensor_copy", xsb[:, :, :], xsT[:, :, :]), reads=(xsT,), writes=(xsb,))
        for q in range(4):
            xi, ac = xin[q % 2], acc[q % 2]
            nm = ("mB0", "mB1", "mC0", "mC1")[q]
            dstt = BT if q < 2 else CTt
            load_halo(S, xi, hT, RB_OFF[nm][0], 128, t, d_hT)
            conv_block(S, cwbc, xi, xi, lambda j, q=q: cwbc[:, q, j:j + 1], ac, ac, 128, dstt[:, q % 2, :], dstt, cbbc[:, q:q + 1], cbbc)
        if t == 0:
            dbg(S, c, "xsT", xsT, xsT[:, :, :])
            dbg(S, c, "dt", dt, dt[:, :, :])
            dbg(S, c, "acs", acs, acs[:, :, :])
        chk(c, "conv")
        for nb in range(c.get("nbr", 4)):
            for h in range(8):
                S.add("pe", CALL("transpose", psTb[:, h * 64:(h + 1) * 64], xsb[:, h, nb * 128:(nb + 1) * 128], identb[0:64, 0:64]),
                      reads=(xsb, identb), writes=(psTb,))
            var = c.get("var", "")
            if "noB" not in var:
                for g in range(2):
                    S.add("pe", CALL("transpose", psTb[:, 512 + g * 128:512 + (g + 1) * 128], BT[:, g, nb * 128:(nb + 1) * 128], identb[:, :]),
                          reads=(BT, identb), writes=(psTb,))
            if True:
                S.add("dve", CALL("tensor_copy", xs_tok[:, nb, :], psTb[:, 0:512]), reads=(psTb,), writes=(xs_tok,))
            else:
                S.add("act", CALL("activation", xs_tok[:, nb, :], psTb[:, 0:512], AF.Identity), reads=(psTb,), writes=(xs_tok,))
            if "noB" not in var:
                S.add("dve", CALL("tensor_copy", B_tok[:, nb, :], psTb[:, 512:768]), reads=(psTb,), writes=(B_tok,))
        chk(c, "transp")
        for nb in range(4):
            n = t * 4 + nb
            cols = slice(nb * 128, (nb + 1) * 128)
            for h in range(8):
                eng = "pool"
                S.add(eng, CALL("tensor_scalar", rhsR[:, h, :], tri[:, :], aa[:, n, h:h + 1], 0.0, ALU.mult, ALU.add),
                      reads=(tri, aa), writes=(rhsR,))
            for hh in range(2):
                S.add("pe", CALL("matmul", psR[hh][:, :], ones32[:, :], rhsR[:, hh * 4:(hh + 1) * 4, :].rearrange("p h l -> p (h l)"),
                                                       start=True, stop=True), reads=(ones32, rhsR), writes=(psR[hh],))
            for hh in range(2):
                S.add("act", CALL("activation", ER[:, hh * 4:(hh + 1) * 4, :].rearrange("p h l -> p (h l)"), psR[hh][:, :], AF.Exp),
                      reads=(psR[hh],), writes=(ER,))
            for h in range(8):
                S.add("dve", CALL("tensor_scalar", Eh[:, h, :], Rap(h), acs[:, n, h:h + 1], 0.0, ALU.subtract, ALU.min),
                      reads=(psR[h // 4], acs), writes=(Eh,))
            S.add("act", CALL("activation", Eh[:, :, :], Eh[:, :, :], AF.Exp), reads=(Eh,), writes=(Eh,))
            chk(c, "RE")
            for g in range(2):
                S.add("pe", CALL("matmul", psG[:, g * 128:(g + 1) * 128], BT[:, g, cols], CTt[:, g, cols], start=True, stop=True),
                      reads=(BT, CTt), writes=(psG,))
            S.add("dve", CALL("tensor_tensor", Gm[:, :, :], psG[:, 0:256].rearrange("p (g l) -> p g l", g=2),
                                                    tri[:, :].unsqueeze(1).to_broadcast([128, 2, 128]), ALU.mult),
                  reads=(psG, tri), writes=(Gm,))
            for g in range(2):
                S.add("dve", CALL("tensor_tensor", Mp[:, g * 4:(g + 1) * 4, :], Eh[:, g * 4:(g + 1) * 4, :],
                                                              Gm[:, g, :].unsqueeze(1).to_broadcast([128, 4, 128]), ALU.mult),
                      reads=(Eh, Gm), writes=(Mp,))
                S.add("dve", CALL("tensor_tensor", Cdec[:, g * 4:(g + 1) * 4, :], ER[:, g * 4:(g + 1) * 4, :],
                                                              CTt[:, g, cols].unsqueeze(1).to_broadcast([128, 4, 128]), ALU.mult),
                      reads=(ER, CTt), writes=(Cdec,))
            S.add("dve", CALL("tensor_tensor", xdt[:, :, :], xs_tok[:, nb, :].rearrange("p (h d) -> p h d", h=8),
                                                    dt[:, n, :].unsqueeze(2).to_broadcast([128, 8, 64]), ALU.mult),
                  reads=(xs_tok, dt), writes=(xdt,))
            for h in range(8):
                py = psY0 if h < 4 else psY1
                oc = slice((h % 4) * 128, (h % 4 + 1) * 128)
                S.add("pe", CALL("matmul", py[0:64, oc], xdt[:, h, :], Mp[:, h, :], start=True, stop=False),
                      reads=(xdt, Mp), writes=(py,))
                S.add("pe", CALL("matmul", py[0:64, oc], Hbf[:, h, :], Cdec[:, h, :], start=False, stop=True),
                      reads=(Hbf, Cdec), writes=(py,))
            S.add("act", CALL("activation", ysb[:, 0:4, cols], psY0[0:64, :].rearrange("p (h l) -> p h l", h=4), AF.Identity),
                  reads=(psY0,), writes=(ysb,))
            S.add("act", CALL("activation", ysb[:, 4:8, cols], psY1[0:64, :].rearrange("p (h l) -> p h l", h=4), AF.Identity),
                  reads=(psY1,), writes=(ysb,))
            if n == 0:
                dbg(S, c, "ER", ER, ER[:, :, :])
                dbg(S, c, "Eh", Eh, Eh[:, :, :])
                dbg(S, c, "Gm", Gm, Gm[:, :, :])
                dbg(S, c, "xs_tok", xs_tok, xs_tok[:, 0, :])
            if n == 1:
                dbg(S, c, "H1", H, H[:, :, :])
            chk(c, "Y")
            for hh in range(2):
                S.add("dve", CALL("tensor_tensor", wv[:, hh * 4:(hh + 1) * 4],
                                                               psR[hh][:, :].rearrange("p (h l) -> p h l", h=4)[:, :, 127],
                                                               acs[:, n, hh * 4:(hh + 1) * 4], ALU.subtract),
                      reads=(psR[hh], acs), writes=(wv,))
            S.add("act", CALL("activation", wv[:, :], wv[:, :], AF.Exp), reads=(wv,), writes=(wv,))
            S.add("dve", CALL("tensor_tensor", wv[:, :], wv[:, :], dt[:, n, :], ALU.mult), reads=(wv, dt), writes=(wv,))
            S.add("dve", CALL("tensor_tensor", xsw[:, :, :], xs_tok[:, nb, :].rearrange("p (h d) -> p h d", h=8),
                                                    wv[:, :].unsqueeze(2).to_broadcast([128, 8, 64]), ALU.mult),
                  reads=(xs_tok, wv), writes=(xsw,))
            for g in range(2):
                S.add("pe", CALL("matmul", psH[:, g * 256:(g + 1) * 256], B_tok[:, nb, g * 128:(g + 1) * 128],
                                                     xsw[:, g * 4:(g + 1) * 4, :].rearrange("p h d -> p (h d)"), start=True, stop=True),
                      reads=(B_tok, xsw), writes=(psH,))
            S.add("dve", CALL("tensor_tensor", H[:, :, :], H[:, :, :], ER[:, :, 127:128].to_broadcast([128, 8, 64]), ALU.mult),
                  reads=(H, ER), writes=(H,))
            S.add("dve", CALL("tensor_tensor", H[:, :, :], H[:, :, :], psH[:, :].rearrange("p (h d) -> p h d", h=8), ALU.add),
                  reads=(H, psH), writes=(H,))
            S.add("act", CALL("activation", Hbf[:, :, :], H[:, :, :], AF.Identity), reads=(H,), writes=(Hbf,))
        if t == 0:
            dbg(S, c, "ysb", ysb, ysb[:, :, :])
        chk(c, "state")
        for h in range(8):
            S.add("dve", CALL("scalar_tensor_tensor", ysb[:, h, :], xsT[:, h, :], Dt[:, h:h + 1], ysb[:, h, :], ALU.mult, ALU.add),
                  reads=(xsT, Dt, ysb), writes=(ysb,))
            mz = mzs[h % 2]
            r0 = RB_OFF[f"mz{h}"][0]
            S.dma("sp", mz, mz[:, :], hT[r0:r0 + 64, t * 512:(t + 1) * 512], reads=(d_hT,))
            S.add("act", CALL("activation", mz[:, :], mz[:, :], AF.Silu), reads=(mz,), writes=(mz,))
            S.add("pool", CALL("tensor_tensor", ysb[:, h, :], ysb[:, h, :], mz[:, :], ALU.mult), reads=(ysb, mz), writes=(ysb,))
        for g in range(2):
            for hh in range(4):
                h = g * 4 + hh
                yq = ysq[hh % 2]
                S.add("act", CALL("activation", yq[:, :], ysb[:, h, :], AF.Square), reads=(ysb,), writes=(yq,))
                S.add("pe", CALL("matmul", psS[0:64, :], ones32[0:64, 0:64], yq[:, :], start=(hh == 0), stop=(hh == 3)),
                      reads=(ones32, yq), writes=(psS,))
            S.add("act", CALL("activation", rs[:, :], psS[0:64, :], AF.Ln, bias=epsT[0:64, 1:2], scale=1.0 / 256), reads=(psS, epsT), writes=(rs,))
            S.add("act", CALL("activation", rs[:, :], rs[:, :], AF.Exp, scale=-0.5), reads=(rs,), writes=(rs,))
            for hh in range(4):
                h = g * 4 + hh
                yt = yo[hh % 2]
                S.add("dve", CALL("scalar_tensor_tensor", yt[:, :], ysb[:, h, :], nw[:, h:h + 1], rs[:, :], ALU.mult, ALU.mult),
                      reads=(ysb, nw, rs), writes=(yt,))
                S.dma("sp", d_cat, catT[256 + h * 64:256 + (h + 1) * 64, t * 512:(t + 1) * 512], yt[:, :], reads=(yt,))


GC = dict(tri=0, triBD=128, blkones=256, ident=384, mbU=512, mbL=640, cmask=768, mb3=772)
GCW = 804


def g_consts():
    s = np.arange(128)[:, None]
    l = np.arange(128)[None, :]
    same = (s // 32) == (l // 32)
    out = np.zeros((128, GCW), np.float32)
    out[:, 0:128] = (s <= l)
    out[:, 128:256] = (s <= l) & same
    out[:, 256:384] = same
    out[:, 384:512] = np.eye(128)
    out[:, 512:640] = np.where((s < l) & same, 0.0, -30000.0)
    out[:, 640:768] = np.where((l < s) & same, 0.0, 30000.0)
    out[:, 768:772] = (np.arange(128)[:, None] // 32) == np.arange(4)[None, :]
    a = np.arange(32)
    out[0:32, 772:804] = np.where(a[:, None] <= a[None, :], 0.0, -30000.0)
    return out


def pack_gdn(P):
    cw = P["g_conv_w"]
    out = {}
    out["g_cw"] = np.ascontiguousarray(cw.reshape(4, 12, 64).transpose(2, 1, 0))
    out["g_dtb"] = np.ascontiguousarray(P["g_dt_bias"].reshape(1, 4))
    out["g_alog"] = np.ascontiguousarray(P["g_A_log"].reshape(1, 4))
    out["g_nw"] = np.ascontiguousarray(P["g_norm_w"].reshape(64, 1))
    return out


def phase_G(S, c, hT, sm_tm, catT, W, gsm_d, T):
    ps = c["ps"]
    psTb = c["psTb"]
    d_hT, d_sm, d_cat = c["d_hT"], c["d_sm"], c["d_cat"]
    d_gsm = c["d_gsm"]
    NB = T // 128
    NC = T // 32
    ntile = T // 512
    ones32, epsT, identb = c["ones32"], c["epsT"], c["identb"]
    gc = c["gconst"]
    def G(name, w=128, rows=128):
        o = GC[name]
        return gc[0:rows, o:o + w]
    gcw, gnw = c["g_cw"], c["g_nw"]
    sm128 = c["sm128"]
    bc = c["g_bc"]
    gg, beta, lnb, gcs, c1, glast, negb, c2 = (c[k] for k in ("g_g", "g_beta", "g_lnb", "g_gcs", "g_c1", "g_glast", "g_negb", "g_c2"))
    gstage = c["g_stage"]
    sm32 = c["g_sm32"]
    dcb = c["g_dcb"]
    rhs16 = c["g_rhs16"]
    xin, acc = c["m_xin"], c["m_acc"]
    qT, kT, vT = c["g_qT"], c["g_kT"], c["g_vT"]
    qTb, kTb, vTb = c["g_qTb"], c["g_kTb"], c["g_vTb"]
    sq = c["g_sq"]
    rhsR = c["g_rhsR"]
    rhsR2 = c["g_rhsR2"]
    E1, E2 = c["g_E1"], c["g_E2"]
    X = c["g_X"]
    XT = c["g_XT"]
    TTf = c["g_TT"]
    Tb = c["g_Tb"]
    ktok, vtok = c["g_ktok"], c["g_vtok"]
    kbg, vb = c["g_kbg"], c["g_vb"]
    wTb = c["g_wTb"]
    ERg = c["g_ERg"]
    qd = c["g_qd"]
    a3 = c["g_a3"]
    attc = c["g_attc"]
    kdec = c["g_kdec"]
    usb = c["g_usb"]
    vnb = c["g_vnb"]
    St = c["g_S"]
    Sbf = c["g_Sbf"]
    oT = c["g_oT"]
    rs = c["g_rs"]
    go = c["g_go"]

    S.dma("sp", gcw, gcw[:, :, :], W["g_cw"])
    S.dma("sp", gnw, gnw[:, :], W["g_nw"])
    S.dma("sp", bc, bc[:, 0:4], W["g_dtb"].partition_broadcast(128))
    S.dma("sp", bc, bc[:, 4:8], W["g_alog"].partition_broadcast(128))
    S.dma("sp", sm128, sm128[:, :, :], sm_tm.rearrange("(n p) c -> p n c", p=128), reads=(d_sm,))
    S.add("act", CALL("activation", bc[:, 4:8], bc[:, 4:8], AF.Exp), reads=(bc,), writes=(bc,))
    S.add("dve", CALL("tensor_scalar", bc[:, 4:8], bc[:, 4:8], -1.0, 0.0, ALU.mult, ALU.add), reads=(bc,), writes=(bc,))
    S.add("act", CALL("activation", beta[:, :, :], sm128[:, :, 8:12], AF.Exp, scale=-1.0), reads=(sm128,), writes=(beta,))
    S.add("act", CALL("activation", lnb[:, :, :], beta[:, :, :], AF.Ln, bias=1.0), reads=(beta,), writes=(lnb,))
    S.add("dve", CALL("tensor_scalar", lnb[:, :, :], lnb[:, :, :], -1.0, 0.0, ALU.mult, ALU.add), reads=(lnb,), writes=(lnb,))
    S.add("dve", CALL("tensor_scalar", beta[:, :, :], beta[:, :, :], 1.0, 0.0, ALU.add, ALU.add), reads=(beta,), writes=(beta,))
    S.add("dve", CALL("reciprocal", beta[:, :, :], beta[:, :, :]), reads=(beta,), writes=(beta,))
    S.add("dve", CALL("tensor_scalar", negb[:, :, :], beta[:, :, :], -1.0, 0.0, ALU.mult, ALU.add), reads=(beta,), writes=(negb,))
    S.add("dve", CALL("tensor_tensor", gg[:, :, :], sm128[:, :, 12:16], bc[:, 0:4].unsqueeze(1).to_broadcast([128, NB, 4]), ALU.add),
          reads=(sm128, bc), writes=(gg,))
    S.add("act", CALL("activation", gg[:, :, :], gg[:, :, :], AF.Exp), reads=(gg,), writes=(gg,))
    S.add("act", CALL("activation", gg[:, :, :], gg[:, :, :], AF.Ln, bias=1.0), reads=(gg,), writes=(gg,))
    S.add("dve", CALL("tensor_tensor", gg[:, :, :], gg[:, :, :], bc[:, 4:8].unsqueeze(1).to_broadcast([128, NB, 4]), ALU.mult),
          reads=(gg, bc), writes=(gg,))
    S.add("pe", CALL("matmul", ps[0][:, 0:NB * 4], G("triBD"), gg[:, :, :].rearrange("p n h -> p (n h)"), start=True, stop=True),
          reads=(gc, gg), writes=(ps[0],))
    S.add("pe", CALL("matmul", ps[1][:, 0:NB * 4], G("blkones"), gg[:, :, :].rearrange("p n h -> p (n h)"), start=True, stop=True),
          reads=(gc, gg), writes=(ps[1],))
    S.add("dve", CALL("tensor_copy", gcs[:, :, :].rearrange("p n h -> p (n h)"), ps[0][:, 0:NB * 4]), reads=(ps[0],), writes=(gcs,))
    S.add("dve", CALL("tensor_copy", glast[:, :, :].rearrange("p n h -> p (n h)"), ps[1][:, 0:NB * 4]), reads=(ps[1],), writes=(glast,))
    S.add("act", CALL("activation", c1[:, :, :], gcs[:, :, :], AF.Exp), reads=(gcs,), writes=(c1,))
    S.add("dve", CALL("tensor_tensor", c1[:, :, :], c1[:, :, :], beta[:, :, :], ALU.mult), reads=(c1, beta), writes=(c1,))
    S.add("dve", CALL("tensor_tensor", c2[:, :, :], glast[:, :, :], gcs[:, :, :], ALU.subtract), reads=(glast, gcs), writes=(c2,))
    S.add("act", CALL("activation", c2[:, :, :], c2[:, :, :], AF.Exp), reads=(c2,), writes=(c2,))
    S.add("dve", CALL("tensor_copy", gstage[:, :, 0:4], gcs[:, :, :]), reads=(gcs,), writes=(gstage,))
    S.add("dve", CALL("tensor_copy", gstage[:, :, 4:8], c2[:, :, :]), reads=(c2,), writes=(gstage,))
    S.dma("sp", d_gsm, gsm_d.rearrange("(n p) c -> p n c", p=128), gstage[:, :, :], reads=(gstage,))
    S.dma("sp", sm32, sm32[:, :, :], gsm_d.rearrange("(n p) c -> p n c", p=32), reads=(d_gsm,))
    S.add("dve", CALL("tensor_tensor", rhs16[:, :, :, :], gg[:, :, :].unsqueeze(2).to_broadcast([128, NB, 4, 4]),
                                            G("cmask", 4).unsqueeze(1).unsqueeze(3).to_broadcast([128, NB, 4, 4]), ALU.mult),
          reads=(gg, gc), writes=(rhs16,))
    for q0 in range(0, NB * 16, 512):
        q1 = min(NB * 16, q0 + 512)
        S.add("pe", CALL("matmul", ps[2][:, 0:q1 - q0], ones32[:, :], rhs16[:, :, :, :].rearrange("p n c h -> p (n c h)")[:, q0:q1],
                                                     start=True, stop=True), reads=(ones32, rhs16), writes=(ps[2],))
        S.add("act", CALL("activation", dcb[:, :, :, :].rearrange("p n c h -> p (n c h)")[:, q0:q1], ps[2][:, 0:q1 - q0], AF.Exp),
              reads=(ps[2],), writes=(dcb,))
    S.add("pool", CALL("memset", St[:, :, :], 0.0), writes=(St,))
    S.add("pool", CALL("memset", Sbf[:, :, :], 0.0), writes=(Sbf,))
    chk(c, "gsmalls")

    psRg, psRgb, psKK, psA, psAT, psT, psX = ps[0], ps[1], ps[2], ps[3], ps[4], ps[5], ps[6]
    psW, psO, psSS, psAt, pswT = ps[1], ps[2], ps[3], ps[4], ps[5]

    def h4(t):
        return t[:, :].rearrange("p (h l) -> p h l", h=4)

    for t in range(ntile):
        for j in range(12):
            xi, ac = xin[j % 2], acc[j % 2]
            nm = ("gq", "gk", "gv")[j // 4] + str(j % 4)
            dst = (qT, kT, vT)[j // 4]
            load_halo(S, xi, hT, RB_OFF[nm][0], 64, t, d_hT)
            conv_block(S, gcw, xi, xi, lambda jj, j=j: gcw[:, j, jj:jj + 1], ac, ac, 64, dst[:, j % 4, :], dst, None, None)
        for qi, (src, dstb, scl) in enumerate(((qT, qTb, 0.125), (kT, kTb, 1.0))):
            for h in range(4):
                s_ = sq[h % 2]
                S.add("act", CALL("activation", s_[:, :], src[:, h, :], AF.Square), reads=(src,), writes=(s_,))
                S.add("pe", CALL("matmul", psX[0:64, :], ones32[0:64, 0:64], s_[:, :], start=True, stop=True),
                      reads=(ones32, s_), writes=(psX,))
                S.add("act", CALL("activation", s_[:, :], psX[0:64, :], AF.Ln, bias=epsT[0:64, 2:3]), reads=(psX, epsT), writes=(s_,))
                S.add("act", CALL("activation", s_[:, :], s_[:, :], AF.Exp, scale=-0.5), reads=(s_,), writes=(s_,))
                S.add("dve", CALL("scalar_tensor_tensor",
                    dstb[:, h, :], src[:, h, :], scl, s_[:, :], ALU.mult, ALU.mult), reads=(src, s_), writes=(dstb,))
        S.add("pool", CALL("tensor_copy", vTb[:, :, :], vT[:, :, :]), reads=(vT,), writes=(vTb,))
        chk(c, "gconv")
        for nb in range(4):
            n = t * 4 + nb
            cols = slice(nb * 128, (nb + 1) * 128)
            for h in range(4):
                S.add("pool", CALL("tensor_scalar", rhsR[:, h, :], G("triBD"), gg[:, n, h:h + 1], 0.0, ALU.mult, ALU.add),
                      reads=(gc, gg), writes=(rhsR,))
                S.add("dve", CALL("scalar_tensor_tensor", rhsR2[:, h, :], G("ident"), lnb[:, n, h:h + 1], rhsR[:, h, :], ALU.mult, ALU.add),
                      reads=(gc, lnb, rhsR), writes=(rhsR2,))
            S.add("pe", CALL("matmul", psRg[:, :], ones32[:, :], rhsR[:, :, :].rearrange("p h l -> p (h l)"), start=True, stop=True),
                  reads=(ones32, rhsR), writes=(psRg,))
            S.add("pe", CALL("matmul", psRgb[:, :], ones32[:, :], rhsR2[:, :, :].rearrange("p h l -> p (h l)"), start=True, stop=True),
                  reads=(ones32, rhsR2), writes=(psRgb,))
            for h in range(4):
                S.add("pe", CALL("matmul", psKK[:, h * 128:(h + 1) * 128], kTb[:, h, cols], kTb[:, h, cols], start=True, stop=True),
                      reads=(kTb,), writes=(psKK,))
            for h in range(4):
                S.add("dve", CALL("scalar_tensor_tensor", E2[:, h, :], psRgb[:, h * 128:(h + 1) * 128], gcs[:, n, h:h + 1], G("mbU"),
                                                                    ALU.subtract, ALU.add), reads=(psRgb, gcs, gc), writes=(E2,))
                S.add("dve", CALL("scalar_tensor_tensor", E1[:, h, :], psRg[:, h * 128:(h + 1) * 128], gcs[:, n, h:h + 1], G("mbL"),
                                                                    ALU.subtract, ALU.add), reads=(psRg, gcs, gc), writes=(E1,))
            S.add("act", CALL("activation", E2[:, :, :], E2[:, :, :], AF.Exp), reads=(E2,), writes=(E2,))
            S.add("act", CALL("activation", E1[:, :, :], E1[:, :, :], AF.Exp, scale=-1.0), reads=(E1,), writes=(E1,))
            x0, xt0 = X[0], XT[0]
            S.add("dve", CALL("scalar_tensor_tensor", xt0[:, :, :], h4(psKK), -1.0, E2[:, :, :], ALU.mult, ALU.mult),
                  reads=(psKK, E2), writes=(xt0,))
            for h in range(4):
                S.add("dve", CALL("scalar_tensor_tensor", x0[:, h, :], psKK[:, h * 128:(h + 1) * 128], negb[:, n, h:h + 1], E1[:, h, :],
                                                                    ALU.mult, ALU.mult), reads=(psKK, negb, E1), writes=(x0,))
            S.add("pool", CALL("tensor_tensor", TTf[:, :, :], xt0[:, :, :], G("ident").unsqueeze(1).to_broadcast([128, 4, 128]), ALU.add),
                  reads=(xt0, gc), writes=(TTf,))
            for j in range(4):
                xa, xta = X[j % 2], XT[j % 2]
                xb, xtb = X[(j + 1) % 2], XT[(j + 1) % 2]
                for h in range(4):
                    S.add("pe", CALL("matmul", psA[:, h * 128:(h + 1) * 128], xta[:, h, :], xa[:, h, :], start=True, stop=True),
                          reads=(xa, xta), writes=(psA,))
                for h in range(4):
                    S.add("pe", CALL("matmul", psAT[:, h * 128:(h + 1) * 128], xa[:, h, :], xta[:, h, :], start=True, stop=True),
                          reads=(xa, xta), writes=(psAT,))
                S.add("act", CALL("activation", xb[:, :, :], h4(psA), AF.Identity), reads=(psA,), writes=(xb,))
                S.add("dve", CALL("tensor_copy", xtb[:, :, :], h4(psAT)), reads=(psAT,), writes=(xtb,))
                for h in range(4):
                    S.add("pe", CALL("matmul", psT[:, h * 128:(h + 1) * 128], xb[:, h, :], TTf[:, h, :], start=True, stop=True),
                          reads=(xb, TTf), writes=(psT,))
                S.add("dve", CALL("tensor_tensor", TTf[:, :, :], TTf[:, :, :], h4(psT), ALU.add), reads=(TTf, psT), writes=(TTf,))
            S.add("act", CALL("activation", Tb[:, :, :], TTf[:, :, :], AF.Identity), reads=(TTf,), writes=(Tb,))
            chk(c, "gsolve")
            for h in range(4):
                S.add("pe", CALL("transpose", psTb[:, h * 64:(h + 1) * 64], kTb[:, h, cols], identb[0:64, 0:64]), reads=(kTb, identb), writes=(psTb,))
            for h in range(4):
                S.add("pe", CALL("transpose", psTb[:, 256 + h * 64:256 + (h + 1) * 64], vTb[:, h, cols], identb[0:64, 0:64]), reads=(vTb, identb), writes=(psTb,))
            S.add("dve", CALL("tensor_tensor", kbg[:, :, :], psTb[:, 0:256].rearrange("p (h d) -> p h d", h=4),
                                                    c1[:, n, :].unsqueeze(2).to_broadcast([128, 4, 64]), ALU.mult), reads=(psTb, c1), writes=(kbg,))
            S.add("dve", CALL("tensor_tensor", vb[:, :, :], psTb[:, 256:512].rearrange("p (h d) -> p h d", h=4),
                                                    beta[:, n, :].unsqueeze(2).to_broadcast([128, 4, 64]), ALU.mult), reads=(psTb, beta), writes=(vb,))
            for h in range(4):
                S.add("pe", CALL("matmul", pswT[0:64, h * 128:(h + 1) * 128], kbg[:, h, :], Tb[:, h, :], start=True, stop=True),
                      reads=(kbg, Tb), writes=(pswT,))
            S.add("act", CALL("activation", wTb[:, :, :], pswT[0:64, :].rearrange("p (h l) -> p h l", h=4), AF.Identity), reads=(pswT,), writes=(wTb,))
            S.add("act", CALL("activation", ERg[:, :, :], psRg[0:64, :].rearrange("p (h l) -> p h l", h=4), AF.Exp), reads=(psRg,), writes=(ERg,))
            S.add("dve", CALL("tensor_tensor", qd[:, :, :], qTb[:, :, cols], ERg[:, :, :], ALU.mult), reads=(qTb, ERg), writes=(qd,))
            chk(c, "gprep")
            for cc in range(4):
                ch = n * 4 + cc
                cs_ = slice(nb * 128 + cc * 32, nb * 128 + cc * 32 + 32)
                cb_ = slice(cc * 32, cc * 32 + 32)
                for h in range(4):
                    S.add("pe", CALL("matmul", psAt[0:32, h * 32:(h + 1) * 32], kTb[:, h, cs_], qTb[:, h, cs_], start=True, stop=True),
                          reads=(kTb, qTb), writes=(psAt,))
                S.add("dve", CALL("tensor_tensor", a3[:, :, :], psRg[0:32, :].rearrange("p (h l) -> p h l", h=4)[:, :, cb_],
                                                        sm32[:, ch, 0:4].unsqueeze(2).to_broadcast([32, 4, 32]), ALU.subtract),
                      reads=(psRg, sm32), writes=(a3,))
                S.add("pool", CALL("tensor_tensor", a3[:, :, :], a3[:, :, :], G("mb3", 32, 32).unsqueeze(1).to_broadcast([32, 4, 32]), ALU.add),
                      reads=(a3, gc), writes=(a3,))
                S.add("act", CALL("activation", a3[:, :, :], a3[:, :, :], AF.Exp), reads=(a3,), writes=(a3,))
                S.add("dve", CALL("tensor_tensor", attc[:, :, :], psAt[0:32, 0:128].rearrange("p (h l) -> p h l", h=4), a3[:, :, :], ALU.mult),
                      reads=(psAt, a3), writes=(attc,))
                for h in range(4):
                    S.add("pe", CALL("transpose", psTb[0:32, 512 + h * 64:512 + (h + 1) * 64], kTb[:, h, cs_], identb[0:64, 0:64]),
                          reads=(kTb, identb), writes=(psTb,))
                S.add("dve", CALL("tensor_tensor", kdec[:, :, :], psTb[0:32, 512:768].rearrange("p (h d) -> p h d", h=4),
                                                        sm32[:, ch, 4:8].unsqueeze(2).to_broadcast([32, 4, 64]), ALU.mult),
                      reads=(psTb, sm32), writes=(kdec,))
                for h in range(4):
                    S.add("pe", CALL("matmul", psW[0:32, 256 + h * 64:256 + (h + 1) * 64], Tb[:, h, cb_], vb[:, h, :], start=True, stop=True),
                          reads=(Tb, vb), writes=(psW,))
                S.add("dve", CALL("tensor_copy", usb[:, :, :], psW[0:32, 256:512].rearrange("p (h d) -> p h d", h=4)), reads=(psW,), writes=(usb,))
                for h in range(4):
                    S.add("pe", CALL("matmul", psW[0:32, h * 64:(h + 1) * 64], wTb[:, h, cb_], Sbf[:, h, :], start=True, stop=True),
                          reads=(wTb, Sbf), writes=(psW,))
                S.add("dve", CALL("tensor_tensor", vnb[:, :, :], usb[:, :, :], psW[0:32, 0:256].rearrange("p (h d) -> p h d", h=4), ALU.subtract),
                      reads=(usb, psW), writes=(vnb,))
                for h in range(4):
                    oc = slice(h * 128 + cc * 32, h * 128 + cc * 32 + 32)
                    S.add("pe", CALL("matmul", psO[0:64, oc], Sbf[:, h, :], qd[:, h, cb_], start=True, stop=False),
                          reads=(Sbf, qd), writes=(psO,))
                    S.add("pe", CALL("matmul", psO[0:64, oc], vnb[:, h, :], attc[:, h, :], start=False, stop=True),
                          reads=(vnb, attc), writes=(psO,))
                for h in range(4):
                    S.add("pe", CALL("matmul", psSS[0:64, h * 64:(h + 1) * 64], kdec[:, h, :], vnb[:, h, :], start=True, stop=True),
                          reads=(kdec, vnb), writes=(psSS,))
                S.add("dve", CALL("tensor_tensor", St[:, :, :], St[:, :, :], dcb[0:64, n, cc, :].unsqueeze(2).to_broadcast([64, 4, 64]), ALU.mult),
                      reads=(St, dcb), writes=(St,))
                S.add("dve", CALL("tensor_tensor", St[:, :, :], St[:, :, :], psSS[0:64, 0:256].rearrange("p (h d) -> p h d", h=4), ALU.add),
                      reads=(St, psSS), writes=(St,))
                S.add("act", CALL("activation", Sbf[:, :, :], St[:, :, :], AF.Identity), reads=(St,), writes=(Sbf,))
            S.add("act", CALL("activation", oT[:, :, cols], psO[0:64, :].rearrange("p (h l) -> p h l", h=4), AF.Identity), reads=(psO,), writes=(oT,))
            chk(c, "gscan")
        gzt = c["g_gz4"]
        for h in range(4):
            r0 = RB_OFF[f"gz{h}"][0]
            S.dma("sp", gzt, gzt[:, h, :], hT[r0:r0 + 64, t * 512:(t + 1) * 512], reads=(d_hT,))
        S.add("act", CALL("activation", gzt[:, :, :], gzt[:, :, :], AF.Silu), reads=(gzt,), writes=(gzt,))
        for h in range(4):
            s_ = sq[h % 2]
            S.add("act", CALL("activation", s_[:, :], oT[:, h, :], AF.Square), reads=(oT,), writes=(s_,))
            S.add("pe", CALL("matmul", psX[0:64, :], ones32[0:64, 0:64], s_[:, :], start=True, stop=True), reads=(ones32, s_), writes=(psX,))
            S.add("act", CALL("activation", s_[:, :], psX[0:64, :], AF.Ln, bias=epsT[0:64, 1:2], scale=1.0 / 64), reads=(psX, epsT), writes=(s_,))
            S.add("act", CALL("activation", s_[:, :], s_[:, :], AF.Exp, scale=-0.5), reads=(s_,), writes=(s_,))
            S.add("dve", CALL("tensor_tensor", s_[:, :], s_[:, :], gzt[:, h, :], ALU.mult), reads=(s_, gzt), writes=(s_,))
            gt = go[h % 2]
            S.add("dve", CALL("scalar_tensor_tensor", gt[:, :], oT[:, h, :], gnw[:, 0:1], s_[:, :], ALU.mult, ALU.mult),
                  reads=(oT, gnw, s_), writes=(gt,))
            S.dma("sp", d_cat, catT[768 + h * 64:768 + (h + 1) * 64, t * 512:(t + 1) * 512], gt[:, :], reads=(gt,))


FF = 2816
NBLK = 11
DEPTH = 4


def ln_tile(S, c, x32, hs, gb, ecol):
    ones, ps1, ps2 = c["ones32"], c["ps"][5], c["ps"][6]
    zsq = c["f_zsq"]
    mean, rstd, msq = c["f_mean"], c["f_rstd"], c["f_msq"]
    epsT = c["epsT"]
    zb, onesb = c["f_zb"], c["onesb128"]
    for k in range(8):
        zq, zz = zsq[k % 2], zb[k % 2]
        S.add("act", CALL("activation", zq[:, :], x32[:, k, hs], AF.Square), reads=(x32,), writes=(zq,))
        S.add("pool", CALL("tensor_copy", zz[:, :], x32[:, k, hs]), reads=(x32,), writes=(zz,))
        S.add("pe", CALL("matmul", ps1[:, :], onesb[:, :], zz[:, :], start=(k == 0), stop=(k == 7)), reads=(zz, onesb), writes=(ps1,))
        S.add("pe", CALL("matmul", ps2[:, :], onesb[:, :], zq[:, :], start=(k == 0), stop=(k == 7)), reads=(zq, onesb), writes=(ps2,))
    S.add("dve", CALL("tensor_scalar", mean[:, :], ps1[:, :], 1.0 / 1024, 0.0, ALU.mult, ALU.add), reads=(ps1,), writes=(mean,))
    S.add("dve", CALL("tensor_tensor", msq[:, :], mean[:, :], mean[:, :], ALU.mult), reads=(mean,), writes=(msq,))
    S.add("dve", CALL("scalar_tensor_tensor", rstd[:, :], ps2[:, :], 1.0 / 1024, msq[:, :], ALU.mult, ALU.subtract), reads=(ps2, msq), writes=(rstd,))
    S.add("act", CALL("activation", rstd[:, :], rstd[:, :], AF.Ln, bias=epsT[:, ecol:ecol + 1]), reads=(rstd, epsT), writes=(rstd,))
    S.add("act", CALL("activation", rstd[:, :], rstd[:, :], AF.Exp, scale=-0.5), reads=(rstd,), writes=(rstd,))
    for k in range(8):
        eng = "dve" if k % 2 == 0 else "pool"
        S.add(eng, CALL("tensor_tensor", x32[:, k, hs], x32[:, k, hs], mean[:, :], ALU.subtract), reads=(x32, mean), writes=(x32,))
        S.add(eng, CALL("tensor_tensor", x32[:, k, hs], x32[:, k, hs], rstd[:, :], ALU.mult), reads=(x32, rstd), writes=(x32,))
        S.add("act", CALL("activation", x32[:, k, hs], x32[:, k, hs], AF.Identity, bias=gb[:, 8 + k:9 + k], scale=gb[:, k:k + 1]),
              reads=(x32, gb), writes=(x32,))


def ffn_stage(S, c, xin, d_in, xout, d_out, wgu_l, wd_l, gb_ap, T):
    x32s, xbfs, act = c["f_x32"], c["f_xbf"], c["f_act"]
    wgu, wd = c["f_wgu"], c["f_wd"]
    ps = c["ps"]
    pg, pu, py = [ps[0], ps[1]], [ps[2], ps[3]], [ps[4], ps[0]]
    sg = c["f_sg"]
    gb = c["f_gb"]
    xin_v = xin.rearrange("(k p) t -> p k t", p=128)
    xout_v = xout.rearrange("(k p) t -> p k t", p=128)
    S.dma("sp", gb, gb[:, :], gb_ap)
    nst = T // 1024

    def load_x(st):
        x32, xbf = x32s[st % 2], xbfs[st % 2]
        S.dma("sp", x32, x32[:, :, :], xin_v[:, :, st * 1024:st * 1024 + 1024], reads=(d_in,))
        for k in range(8):
            S.dma("pool", xbf, xbf[:, k, :], xin_v[:, k, st * 1024:st * 1024 + 1024], reads=(d_in,))

    load_x(0)
    for st in range(nst):
        T0 = st * 1024
        x32, xbf = x32s[st % 2], xbfs[st % 2]
        if st + 1 < nst:
            load_x(st + 1)
        it = 0
        for blk in range(NBLK):
            wt = wgu[blk % 2]
            S.dma("sp", wt, wt[:, :, :], wgu_l[blk, :, :, :], reads=(c["d_wbf"],))
            for jj in range(2):
                j = blk * 2 + jj
                for half in range(2):
                    tg, tu = pg[it % 2], pu[it % 2]
                    tsg = sg[it % 2]
                    it += 1
                    hs = slice(half * 512, half * 512 + 512)
                    for k in range(8):
                        S.add("pe", CALL("matmul", tg[:, :], wt[:, k, jj * 128:(jj + 1) * 128], xbf[:, k, hs], start=(k == 0), stop=(k == 7)),
                              reads=(wt, xbf), writes=(tg,))
                    for k in range(8):
                        S.add("pe", CALL("matmul", tu[:, :], wt[:, k, 256 + jj * 128:256 + (jj + 1) * 128], xbf[:, k, hs], start=(k == 0), stop=(k == 7)),
                              reads=(wt, xbf), writes=(tu,))
                    S.add("act", CALL("activation", tsg[:, :], tg[:, :], AF.Silu), reads=(tg,), writes=(tsg,))
                    S.add("dve", CALL("tensor_tensor", act[:, j, hs], tsg[:, :], tu[:, :], ALU.mult), reads=(tsg, tu), writes=(act,))
        it = 0
        for cc in range(8):
            wt = wd[cc % 2]
            S.dma("sp", wt, wt[:, :, :], wd_l[cc, :, :, :], reads=(c["d_wbf"],))
            for half in range(2):
                ty = ps[4] if it % 2 == 0 else ps[1]
                it += 1
                hs = slice(half * 512, half * 512 + 512)
                for j in range(22):
                    S.add("pe", CALL("matmul", ty[:, :], wt[:, j, :], act[:, j, hs], start=(j == 0), stop=(j == 21)), reads=(wt, act), writes=(ty,))
                S.add("dve", CALL("scalar_tensor_tensor", x32[:, cc, hs], x32[:, cc, hs], 2.0 * ALPHA, ty[:, :], ALU.mult, ALU.add),
                      reads=(ty, x32), writes=(x32,))
        for half in range(2):
            hs = slice(half * 512, half * 512 + 512)
            ln_tile(S, c, x32, hs, gb, 0)
            S.dma("sp", d_out, xout_v[:, :, T0 + half * 512:T0 + half * 512 + 512], x32[:, :, hs], reads=(x32,))


def phase_O(S, c, catT, x1T, d_x1, xout, d_out, wout_l, gb_ap, T):
    ps = c["ps"]
    wo = c["o_w"]
    cat = c["o_cat"]
    x32s = c["o_x32"]
    gb = c["f_gb"]
    d_cat = c["d_cat"]
    catv = catT.rearrange("(k p) t -> p k t", p=128)
    x1v = x1T.rearrange("(k p) t -> p k t", p=128)
    xov = xout.rearrange("(k p) t -> p k t", p=128)
    S.dma("sp", gb, gb[:, :], gb_ap)
    S.dma("sp", wo, wo[:, :, :], wout_l[:, :, :], reads=(c["d_wbf"],))
    nt = T // 512

    def proj(t):
        ct, x32 = cat[t % 2], x32s[t % 3]
        ts = slice(t * 512, (t + 1) * 512)
        S.dma("sp", ct, ct[:, :, :], catv[:, :, ts], reads=(d_cat,))
        S.dma("sp", x32, x32[:, :, :], x1v[:, :, ts], reads=(d_x1,))
        for cc in range(8):
            ty = ps[cc % 4]
            for k in range(8):
                S.add("pe", CALL("matmul", ty[:, :], wo[:, k, cc * 128:(cc + 1) * 128], ct[:, k, :], start=(k == 0), stop=(k == 7)),
                      reads=(wo, ct), writes=(ty,))
            S.add("dve", CALL("scalar_tensor_tensor", x32[:, cc, :], x32[:, cc, :], ALPHA, ty[:, :], ALU.mult, ALU.add), reads=(ty, x32), writes=(x32,))

    proj(0)
    for t in range(nt):
        if t + 1 < nt:
            proj(t + 1)
        x32 = x32s[t % 3]
        ln_tile(S, c, x32, slice(0, 512), gb, 1)
        S.dma("sp", d_out, xov[:, :, t * 512:(t + 1) * 512], x32[:, :, :], reads=(x32,))


def alloc_all(S, c, T):
    NB = T // 128
    NC = T // 32
    c["ps"] = [S.ps(f"ps{i}", [128, 512]) for i in range(7)]
    c["psTb"] = S.ps("psTb", [128, 1024], BF16)
    c["psTf"] = Tile("psTf", c["psTb"].t[:, :].bitcast(F32))
    c["psTf"].psum = True
    for nm, shp, dt_ in (("ones32", [128, 128], F32), ("epsT", [128, 4], F32), ("tri", [128, 128], F32), ("identb", [128, 128], BF16),
                         ("gconst", [128, GCW], F32)):
        c[nm] = S.sb(nm, shp, dt_)
    c["onesb128"] = S.sb("onesb128", [128, 128], BF16)
    S.make_arena(186 * 1024)
    for n in ("d_hT", "d_vtm", "d_sm", "d_cat", "d_gsm", "d_xa", "d_xb", "d_xc", "d_xin", "d_out"):
        c[n] = Tile(n, None)

    def A(name, shape, dt=F32):
        c[name] = S.av(name, shape, dt)

    def AL(name, n, shape, dt=F32):
        c[name] = [S.av(f"{name}{i}", shape, dt) for i in range(n)]
    for phase in ("F", "O"):
        S.arena_reset()
        A("f_gb", [128, 16]); AL("f_zsq", 2, [128, 512], BF16); AL("f_zb", 2, [128, 512], BF16); A("f_mean", [128, 512]); A("f_rstd", [128, 512]); A("f_msq", [128, 512])
        if phase == "F":
            AL("f_x32", 2, [128, 8, 1024]); AL("f_xbf", 2, [128, 8, 1024], BF16); A("f_act", [128, 22, 1024], BF16)
            AL("f_wgu", 2, [128, 8, 512], BF16); AL("f_wd", 2, [128, 22, 128], BF16); AL("f_sg", 2, [128, 512])
        else:
            A("o_w", [128, 8, 1024], BF16); AL("o_cat", 2, [128, 8, 512], BF16); AL("o_x32", 3, [128, 8, 512])
    S.arena_reset()
    A("b1_xbf", [128, 8, T], BF16); AL("b1_w", 2, [128, 8, 512], BF16); AL("b1_stg", 3, [128, 2048]); A("b1_wtm", [128, 8, 272], BF16)
    S.arena_reset()
    A("a_Kc", [128, 2, T], BF16); A("a_V", [128, T // 128, 128], BF16); A("a_Qall", [128, T // 512, 512], BF16); AL("a_Qp", 4, [128, 512], BF16)
    AL("a_ld", 4, [128, 512]); AL("a_cs", 2, [128, 2, 512]); AL("a_tmp", 2, [128, 512]); AL("a_P", 4, [128, 512], BF16)
    A("a_mask", [128, 4, 512], BF16); A("a_onesb", [128, 64], BF16); A("a_lam", [128, 140]); AL("a_sb64", 6, [64, 512])
    AL("a_aob", 2, [64, 512], BF16); A("a_subw", [64, 1]); AL("a_xo", 2, [128, 512]); A("a_sel", [128, 64])
    for phase in ("M", "G"):
        S.arena_reset()
        A("sm128", [128, NB, 16]); AL("m_xin", 2, [128, 515]); AL("m_acc", 2, [128, 512])
        if phase == "M":
            A("m_cwx", [64, 8, 4]); A("m_cwbc", [128, 4, 4]); A("m_cbx", [64, 8]); A("m_cbbc", [128, 4]); A("m_Dt", [64, 8]); A("m_nw", [64, 8])
            A("m_dt", [128, NB, 8]); A("m_a", [128, NB, 8]); A("m_acs", [128, NB, 8]); A("m_bc", [128, 16])
            A("m_xsT", [64, 8, 512]); A("m_xsb", [64, 8, 512], BF16); A("m_BT", [128, 2, 512], BF16); A("m_CT", [128, 2, 512], BF16)
            A("m_xs_tok", [128, 4, 512], BF16); A("m_B_tok", [128, 4, 256], BF16)
            A("m_rhsR", [128, 8, 128]); A("m_ER", [128, 8, 128]); A("m_E", [128, 8, 128]); A("m_Gm", [128, 2, 128])
            A("m_Mp", [128, 8, 128], BF16); A("m_Cdec", [128, 8, 128], BF16); A("m_xdt", [128, 8, 64], BF16); A("m_xsw", [128, 8, 64], BF16)
            A("m_wv", [128, 8]); A("m_H", [128, 8, 64]); A("m_Hbf", [128, 8, 64], BF16); A("m_ysb", [64, 8, 512])
            AL("m_mz", 2, [64, 512]); AL("m_ysq", 2, [64, 512]); A("m_rs", [64, 512]); AL("m_yo", 2, [64, 512], BF16)
        else:
            A("g_cw", [64, 12, 4]); A("g_nw", [64, 1]); A("g_bc", [128, 8])
            for k in ("g_g", "g_beta", "g_lnb", "g_gcs", "g_c1", "g_glast", "g_negb", "g_c2"):
                A(k, [128, NB, 4])
            A("g_stage", [128, NB, 8]); A("g_sm32", [32, NC, 8]); A("g_dcb", [128, NB, 4, 4]); A("g_rhs16", [128, NB, 4, 4])
            for k in ("g_qT", "g_kT", "g_vT"):
                A(k, [64, 4, 512])
            for k in ("g_qTb", "g_kTb", "g_vTb"):
                A(k, [64, 4, 512], BF16)
            AL("g_sq", 2, [64, 512]); A("g_rhsR", [128, 4, 128]); A("g_rhsR2", [128, 4, 128]); A("g_E1", [128, 4, 128]); A("g_E2", [128, 4, 128])
            AL("g_X", 2, [128, 4, 128]); AL("g_XT", 2, [128, 4, 128]); A("g_TT", [128, 4, 128]); A("g_Tb", [128, 4, 128], BF16)
            for k in ("g_ktok", "g_vtok", "g_kbg", "g_vb"):
                A(k, [128, 4, 64], BF16)
            A("g_wTb", [64, 4, 128], BF16); A("g_ERg", [64, 4, 128]); A("g_qd", [64, 4, 128], BF16)
            A("g_a3", [32, 4, 32]); A("g_attc", [32, 4, 32], BF16); A("g_kdec", [32, 4, 64], BF16); A("g_usb", [32, 4, 64]); A("g_vnb", [32, 4, 64], BF16)
            A("g_S", [64, 4, 64]); A("g_Sbf", [64, 4, 64], BF16); A("g_oT", [64, 4, 512]); A("g_gz4", [64, 4, 512]); A("g_rs", [64, 512]); AL("g_go", 2, [64, 512], BF16)


def pack_ffn(wgu, wd):
    wgu_l = np.empty((NBLK, 128, 8, 512), np.float32)
    wk = wgu.reshape(8, 128, 2 * FF)
    for blk in range(NBLK):
        wgu_l[blk, :, :, 0:256] = wk[:, :, blk * 256:(blk + 1) * 256].transpose(1, 0, 2)
        wgu_l[blk, :, :, 256:512] = wk[:, :, FF + blk * 256:FF + (blk + 1) * 256].transpose(1, 0, 2)
    wd_l = np.ascontiguousarray(wd.reshape(22, 128, 8, 128).transpose(2, 1, 0, 3))
    return wgu_l, wd_l


def pack_gb(g, b):
    return np.ascontiguousarray(np.concatenate([g.reshape(8, 128).T, b.reshape(8, 128).T], 1))


def pack_layer(P):
    out = {}
    out["f1_wgu"], out["f1_wd"] = pack_ffn(P["ffn1_w_gu"], P["ffn1_w_down"])
    out["f2_wgu"], out["f2_wd"] = pack_ffn(P["ffn2_w_gu"], P["ffn2_w_down"])
    out["gb1"] = pack_gb(P["ln1_g"], P["ln1_b"])
    out["gb2"] = pack_gb(P["ln2_g"], P["ln2_b"])
    out["gb3"] = pack_gb(P["ln3_g"], P["ln3_b"])
    out.update(pack_mixer(P))
    out.update(pack_mamba(P))
    out.update(pack_gdn(P))
    out["wout"] = np.ascontiguousarray(P["w_out"].reshape(8, 128, 1024).transpose(1, 0, 2))
    return out


LAYER_KEYS = ("f1_wgu", "f1_wd", "f2_wgu", "f2_wd", "gb1", "gb2", "gb3", "wfm", "wtm", "da_lambda", "subw",
              "m_cwx", "m_cwbc", "m_cbx", "m_cbbc", "m_dtb", "m_alog", "m_D", "m_nw", "g_cw", "g_dtb", "g_alog", "g_nw", "wout")


def pack_all(inputs, nlayers):
    layers = [pack_layer({k: np.asarray(v[l], np.float32) for k, v in inputs.items() if k != "x"}) for l in range(nlayers)]
    return {k: np.ascontiguousarray(np.stack([lay[k] for lay in layers])) for k in LAYER_KEYS}


def build_program(T, nlayers, shapes, lam_inits, first_layer=0):
    nc = bass.Bass("TRN2", target_bir_lowering=False)
    S = Sched(nc)
    aps = {}
    for k, (shp, is_bf) in shapes.items():
        aps[k] = nc.dram_tensor(k, list(shp), BF16 if is_bf else F32, kind="ExternalInput").ap()
    yT = nc.dram_tensor("yT", [D, T], F32, kind="ExternalOutput").ap()
    xa = nc.dram_tensor("xa", [D, T], F32).ap()
    xb = nc.dram_tensor("xb", [D, T], F32).ap()
    xc = nc.dram_tensor("xc", [D, T], F32).ap()
    hT = nc.dram_tensor("hT", [NROWS, T], F32).ap()
    v_tm = nc.dram_tensor("v_tm", [T, 256], F32).ap()
    sm_tm = nc.dram_tensor("sm_tm", [T, 16], F32).ap()
    gsm = nc.dram_tensor("gsm", [T, 8], F32).ap()
    catT = nc.dram_tensor("catT", [D, T], BF16).ap()
    c = {}
    alloc_all(S, c, T)
    S.add("dve", CALL("memset", c["ones32"][:, :], 1.0), writes=(c["ones32"],))
    S.add("dve", CALL("memset", c["onesb128"][:, :], 1.0), writes=(c["onesb128"],))
    S.add("dve", CALL("memset", c["epsT"][:, 0:1], 4 * EPS), writes=(c["epsT"],))
    S.add("dve", CALL("memset", c["epsT"][:, 1:2], EPS), writes=(c["epsT"],))
    S.add("dve", CALL("memset", c["epsT"][:, 2:3], 1e-6), writes=(c["epsT"],))
    gc = c["gconst"]
    S.dma("sp", gc, gc[:, :], aps["gconst"])
    S.add("dve", CALL("tensor_copy", c["tri"][:, :], gc[:, 0:128]), reads=(gc,), writes=(c["tri"],))
    S.add("dve", CALL("tensor_copy", c["identb"][:, :], gc[:, 384:512]), reads=(gc,), writes=(c["identb"],))
    kstop = ""
    BIGW = ("f1_wgu", "f1_wd", "f2_wgu", "f2_wd", "wfm", "wtm", "wout")
    wbf = {}
    d_wbf = Tile("d_wbf", None)
    for kname in BIGW:
        wbf[kname] = nc.dram_tensor(kname + "_bf", list(shapes[kname][0]), BF16).ap()

    def convert_layer(li):
        for kname in BIGW:
            per = 1
            for d_ in shapes[kname][0][1:]:
                per *= d_
            rows = per // 2048
            src = aps[kname][li].flatten().rearrange("(r e) -> r e", e=2048)
            dst = wbf[kname][li].flatten().rearrange("(r e) -> r e", e=2048)
            step = 704 if rows > 704 else rows
            for r0 in range(0, rows, step):
                r1 = min(rows, r0 + step)
                S.dma("pool", d_wbf, dst[r0:r1, :], src[r0:r1, :])

    convert_layer(0)
    c["d_wbf"] = d_wbf
    cur, d_cur = aps["xT"], c["d_xin"]
    for li in range(nlayers):
        W = {k: (wbf[k][li] if k in wbf else aps[k][li]) for k in LAYER_KEYS}
        last = li == nlayers - 1
        S.barrier()
        ffn_stage(S, c, cur, d_cur, xb, c["d_xb"], W["f1_wgu"], W["f1_wd"], W["gb1"], T)
        if kstop == "F1":
            break
        S.barrier()
        phase_B1(S, c, xb, hT, v_tm, sm_tm, W["wfm"], W["wtm"], T)
        if kstop == "B1":
            break
        S.barrier()
        if li + 1 < nlayers:
            convert_layer(li + 1)
        phase_A(S, c, hT, v_tm, catT, aps["CT"], aps["ST"], aps["amask"], W["da_lambda"], W["subw"], lam_inits[li], T)
        if kstop == "A":
            break
        S.barrier()
        phase_M(S, c, hT, sm_tm, catT, W, T)
        if kstop == "M":
            break
        S.barrier()
        phase_G(S, c, hT, sm_tm, catT, W, gsm, T)
        if kstop == "G":
            break
        S.barrier()
        phase_O(S, c, catT, xb, c["d_xb"], xc, c["d_xc"], W["wout"], W["gb2"], T)
        if kstop == "O":
            break
        S.barrier()
        dst, d_dst = (yT, c["d_out"]) if last else (xa, c["d_xa"])
        ffn_stage(S, c, xc, c["d_xc"], dst, d_dst, W["f2_wgu"], W["f2_wd"], W["gb3"], T)
        cur, d_cur = xa, c["d_xa"]
    S.barrier()
    S.add("sp", None, reads=(c["d_out"],))
    st = S.emit()
    return nc, st


def lambda_init(l):
    return 0.8 - 0.6 * math.exp(-0.3 * l)


SEQ = 4096
NCORES = 4


def kernel(**inputs):
    inputs = {k: np.asarray(v) for k, v in inputs.items()}
    x = inputs["x"].astype(np.float32, copy=False)
    packed = pack_all(inputs, DEPTH)
    CT, ST = rope_consts(SEQ)
    common = {"gconst": g_consts(), "CT": CT, "ST": ST, "amask": attn_mask()}
    in_maps = []
    for b in range(NCORES):
        m = dict(packed)
        m.update(common)
        m["xT"] = np.ascontiguousarray(x[b].T)
        in_maps.append(m)
    shapes = {k: (v.shape, v.dtype == ml_dtypes.bfloat16) for k, v in in_maps[0].items()}
    nc, _ = build_program(SEQ, DEPTH, shapes, [lambda_init(l) for l in range(DEPTH)])
    res = run_bass_kernel_spmd(nc, in_maps, core_ids=list(range(NCORES)))
    out = np.stack([np.asarray(res.results[b]["yT"]).T for b in range(NCORES)])
    return np.ascontiguousarray(out.astype(np.float32, copy=False))
```

```python
import math
import numpy as np
import ml_dtypes
import concourse.bass as bass
import concourse.mybir as mybir
from concourse.bass_utils import run_bass_kernel_spmd


F32 = mybir.dt.float32
BF16 = mybir.dt.bfloat16
ALU = mybir.AluOpType
AF = mybir.ActivationFunctionType
AX = mybir.AxisListType


class Tile:
    __slots__ = ("name", "t", "last_w", "readers", "sem", "dma_cnt", "psum")

    def __init__(self, name, t):
        self.name = name
        self.t = t
        self.last_w = None
        self.readers = []
        self.sem = None
        self.dma_cnt = 0
        self.psum = False

    def __getitem__(self, idx):
        return self.t[idx]


class CALL:
    __slots__ = ("name", "args", "kw")

    def __init__(self, name, *args, **kw):
        self.name = name
        self.args = args
        self.kw = kw

    def __call__(self, eng):
        return getattr(eng, self.name)(*self.args, **self.kw)


class Op:
    __slots__ = ("eng", "fn", "deps", "signal", "sigval", "is_dma", "dma_tile", "dma_waits", "idx", "line")


ENGS = ("pe", "act", "dve", "pool", "sp")


class Sched:
    def __init__(self, nc):
        self.nc = nc
        self.ops = []
        self.last_op = {e: None for e in ENGS}
        self.dma_tiles = []
        self.debug = False
        self.log = []

    def sb(self, name, shape, dtype):
        return Tile(name, self.nc.alloc_sbuf_tensor("s_" + name, list(shape), dtype))

    def make_arena(self, nbytes):
        self.arena = self.nc.alloc_sbuf_tensor("s_arena", [128, nbytes // 4], F32)
        self.arena_bytes = nbytes
        self.aptr = 0

    def arena_reset(self):
        self.aptr = 0

    def av(self, name, shape, dtype):
        dsize = 4 if dtype == F32 else 2
        nelem = 1
        for d in shape[1:]:
            nelem *= d
        nbytes = (nelem * dsize + 31) // 32 * 32
        off = self.aptr
        assert off + nbytes <= self.arena_bytes, f"arena overflow at {name}: {off + nbytes} > {self.arena_bytes}"
        self.aptr += nbytes
        ap = self.arena[0:shape[0], off // 4:(off + nbytes) // 4]
        if dtype != F32:
            ap = ap.bitcast(dtype)
        ap = ap[:, 0:nelem]
        if len(shape) == 3:
            ap = ap.rearrange("p (a b) -> p a b", a=shape[1])
        elif len(shape) == 4:
            ap = ap.rearrange("p (a b c) -> p a b c", a=shape[1], b=shape[2])
        return Tile(name, ap)

    def ps(self, name, shape, dtype=F32):
        t = Tile(name, self.nc.alloc_psum_tensor("p_" + name, list(shape), dtype))
        t.psum = True
        return t

    def add(self, eng, fn, reads=(), writes=(), dma_dst=None):
        op = Op()
        op.eng = eng
        op.fn = fn
        op.signal = False
        op.sigval = None
        op.is_dma = dma_dst is not None
        op.dma_tile = dma_dst
        op.idx = len(self.ops)
        if self.debug:
            import sys as _sys
            f = _sys._getframe(1)
            if f.f_code.co_name == "dma":
                f = f.f_back
            op.line = f"{f.f_code.co_name}:{f.f_lineno}"
        else:
            op.line = ""
        deps = {}
        for r in reads:
            if r.last_w is not None:
                deps[id(r.last_w)] = r.last_w
            if r.psum:
                for rd in r.readers:
                    if rd.eng != eng:
                        deps[id(rd)] = rd
        for w in writes:
            if w.last_w is not None:
                deps[id(w.last_w)] = w.last_w
            for rd in w.readers:
                deps[id(rd)] = rd
        deps.pop(id(op), None)
        dl = []
        dma_waits = {}
        for d in deps.values():
            if d.is_dma:
                t = d.dma_tile
                dma_waits[id(t)] = (t, t.dma_cnt)
            elif d.eng == "pe" and eng == "pe" and not op.is_dma:
                continue
            else:
                dl.append(d)
        op.deps = dl
        op.dma_waits = list(dma_waits.values())
        for r in reads:
            r.readers.append(op)
        for w in writes:
            w.last_w = op
            w.readers = []
        if op.is_dma:
            t = dma_dst
            if t.sem is None:
                t.sem = self.nc.alloc_semaphore("d_" + t.name)
                self.dma_tiles.append(t)
            t.dma_cnt += 16
        self.ops.append(op)
        self.last_op[eng] = op
        return op

    def dma(self, q, dst_tile, out_ap, in_ap, reads=(), extra_writes=(), **kw):
        return self.add(q, lambda e: e.dma_start(out=out_ap, in_=in_ap, **kw),
                        reads=reads, writes=(dst_tile,) + tuple(extra_writes), dma_dst=dst_tile)

    def barrier(self):
        lasts = [o for o in self.last_op.values() if o is not None]
        dts = [(t, t.dma_cnt) for t in self.dma_tiles]
        for e in ENGS:
            op = Op()
            op.eng = e
            op.fn = None
            op.signal = False
            op.sigval = None
            op.is_dma = False
            op.dma_tile = None
            op.idx = len(self.ops)
            op.line = "barrier"
            op.deps = [o for o in lasts if not o.is_dma]
            op.dma_waits = list(dts)
            self.ops.append(op)

    def emit(self):
        nc = self.nc
        for op in self.ops:
            for d in op.deps:
                d.signal = True
        cnt = {e: 0 for e in ENGS}
        for op in self.ops:
            if op.signal and not op.is_dma:
                cnt[op.eng] += 1
                op.sigval = cnt[op.eng]
        esem = {e: nc.alloc_semaphore("e_" + e) for e in ENGS}
        per = {e: [o for o in self.ops if o.eng == e] for e in ENGS}
        stats = {"waits": 0, "insts": 0}
        semname = {id(esem[e]): e for e in ENGS}
        for t in self.dma_tiles:
            semname[id(t.sem)] = "d:" + t.name

        def run(ename, eng):
            waited = {}
            for op in per[ename]:
                ws = {}
                for d in op.deps:
                    s = esem[d.eng]
                    ws[id(s)] = (s, max(ws.get(id(s), (None, 0))[1], d.sigval))
                for (t, v) in op.dma_waits:
                    ws[id(t.sem)] = (t.sem, max(ws.get(id(t.sem), (None, 0))[1], v))
                wl = []
                for (s, v) in ws.values():
                    if waited.get(id(s), 0) < v:
                        eng.wait_ge(s, v)
                        waited[id(s)] = v
                        stats["waits"] += 1
                        wl.append((semname.get(id(s), "?"), v))
                if self.debug:
                    sig = ""
                    if op.is_dma:
                        sig = f"-> {op.dma_tile.name}+16"
                    elif op.signal:
                        sig = f"-> {ename}={op.sigval}"
                    self.log.append(f"{ename:5s} #{op.idx:6d} {op.line:28s} waits={wl} {sig}")
                if op.fn is None:
                    continue
                inst = op.fn(eng)
                stats["insts"] += 1
                if op.is_dma:
                    inst.then_inc(op.dma_tile.sem, 16)
                elif op.signal:
                    inst.then_inc(esem[ename], 1)

        with nc.Block() as block:
            @block.tensor
            def _(e):
                run("pe", e)

            @block.scalar
            def _(e):
                run("act", e)

            @block.vector
            def _(e):
                run("dve", e)

            @block.gpsimd
            def _(e):
                run("pool", e)

            @block.sync
            def _(e):
                run("sp", e)
        return stats


D = 1024
EPS = 1e-5
ALPHA = 8.0 ** 0.25

def rowblocks():
    rb = []
    for side in ("q", "k"):
        for c in range(2):
            rb.append((f"a{side}m{c}", 128))
            rb.append((f"a{side}p{c}", 128))
    for h in range(8):
        rb.append((f"mz{h}", 64))
    for h in range(8):
        rb.append((f"mx{h}", 64))
    for nm in ("gq", "gk", "gv", "gz"):
        for h in range(4):
            rb.append((f"{nm}{h}", 64))
    for g in range(2):
        rb.append((f"mB{g}", 128))
    for g in range(2):
        rb.append((f"mC{g}", 128))
    return rb


RB = rowblocks()
RB_OFF = {}
_o = 0
for _n, _m in RB:
    RB_OFF[_n] = (_o, _m)
    _o += _m
NROWS = _o

IN_SIZES = (256, 256, 256, 512, 1024, 8, 256, 256, 256, 256, 4, 4)
IN_OFF = np.concatenate([[0], np.cumsum(IN_SIZES)])


def fm_cols():
    cols = []
    perm = np.arange(32)
    perm[0:4] = np.arange(4, 8)
    perm[4:8] = np.arange(0, 4)
    for si, side in enumerate(("q", "k")):
        base = IN_OFF[si]
        for c in range(2):
            cols.append(np.concatenate([base + hm * 32 + np.arange(32) for hm in range(4 * c, 4 * c + 4)]))
            cols.append(np.concatenate([base + hm * 32 + perm for hm in range(4 * c, 4 * c + 4)]))
    cols.append(IN_OFF[3] + np.arange(512))
    cols.append(IN_OFF[4] + np.arange(512))
    for j in (6, 7, 8, 9):
        cols.append(IN_OFF[j] + np.arange(256))
    cols.append(IN_OFF[4] + 512 + np.arange(256))
    cols.append(IN_OFF[4] + 768 + np.arange(256))
    return np.concatenate(cols)


FM_COLS = fm_cols()
TM_COLS = np.concatenate([IN_OFF[2] + np.arange(256), IN_OFF[5] + np.arange(8), IN_OFF[10] + np.arange(4), IN_OFF[11] + np.arange(4)])
NTM = 272


def rope_consts(T):
    pos = np.arange(T, dtype=np.float32)
    inv = (500000.0 ** (-np.arange(0, 8, 2, dtype=np.float32) / 8.0)).astype(np.float32)
    ang = pos[:, None] * inv[None, :]
    c, s = np.cos(ang).astype(np.float32), np.sin(ang).astype(np.float32)
    CT = np.ones((32, T), np.float32)
    ST = np.zeros((32, T), np.float32)
    CT[0:4] = c.T
    CT[4:8] = c.T
    ST[0:4] = -s.T
    ST[4:8] = s.T
    return np.ascontiguousarray(np.tile(CT, (4, 1))), np.ascontiguousarray(np.tile(ST, (4, 1)))


def attn_mask():
    m = np.zeros((128, 4, 512), np.float32)
    for r in range(4):
        k = r * 128 + np.arange(128)
        q = np.arange(512)
        m[:, r, :] = (k[:, None] // 64) <= (q[None, :] // 64)
    return m.astype(ml_dtypes.bfloat16)


def pack_mixer(P):
    w_in = P["w_in"]
    wfm = w_in[:, FM_COLS]
    wfm_l = np.ascontiguousarray(wfm.reshape(8, 128, NROWS // 512, 512).transpose(2, 1, 0, 3))
    wtm = w_in[:, TM_COLS]
    wtm_l = np.ascontiguousarray(wtm.reshape(8, 128, NTM).transpose(1, 0, 2))
    out = dict(wfm=wfm_l, wtm=wtm_l)
    out["da_lambda"] = np.ascontiguousarray(P["da_lambda"].reshape(1, 128))
    out["subw"] = np.ascontiguousarray(P["da_subln_w"].reshape(64, 1))
    return out


def phase_B1(S, c, x1T, hT, v_tm, sm_tm, wfm_l, wtm_l, T, d_x1=None):
    xbf = c["b1_xbf"]
    wbuf = c["b1_w"]
    stg = c["b1_stg"]
    ps = c["ps"]
    x1v = x1T.rearrange("(k p) t -> p k t", p=128)
    for k in range(8):
        for t0 in range(0, T, 2048):
            t1 = min(T, t0 + 2048)
            S.dma("pool", xbf, xbf[:, k, t0:t1], x1v[:, k, t0:t1])
    ntile = T // 512
    it = 0
    sgi = 0
    for wb in range(NROWS // 512):
        wt = wbuf[wb % 2]
        S.dma("sp", wt, wt[:, :, :], wfm_l[wb, :, :, :], reads=(c["d_wbf"],))
        for col in range(0, 512, 128):
            r0 = wb * 512 + col
            for tg in range(0, ntile, 4):
                sg = stg[sgi % 3]
                sgi += 1
                nt_ = min(4, ntile - tg)
                for tt in range(nt_):
                    t = tg + tt
                    pt = ps[it % 6]
                    it += 1
                    for k in range(8):
                        S.add("pe", CALL("matmul", pt[:, :], wt[:, k, col:col + 128], xbf[:, k, t * 512:(t + 1) * 512],
                                         start=(k == 0), stop=(k == 7)), reads=(wt, xbf), writes=(pt,))
                    if it % 2 == 0:
                        S.add("act", CALL("activation", sg[:, tt * 512:(tt + 1) * 512], pt[:, :], AF.Identity), reads=(pt,), writes=(sg,))
                    else:
                        S.add("dve", CALL("tensor_copy", sg[:, tt * 512:(tt + 1) * 512], pt[:, :]), reads=(pt,), writes=(sg,))
                S.dma("sp", c["d_hT"], hT[r0:r0 + 128, tg * 512:(tg + nt_) * 512], sg[:, 0:nt_ * 512], reads=(sg,))
    wtm = c["b1_wtm"]
    S.dma("sp", wtm, wtm[:, :, :], wtm_l[:, :, :], reads=(c["d_wbf"],))
    for n in range(T // 128):
        pt = ps[n % 6]
        sg = stg[n % 3]
        for k in range(8):
            S.add("pe", CALL("matmul",
                pt[:, 0:NTM], xbf[:, k, n * 128:(n + 1) * 128], wtm[:, k, :], start=(k == 0), stop=(k == 7)),
                reads=(wtm, xbf), writes=(pt,))
        S.add("act", CALL("activation", sg[:, 0:NTM], pt[:, 0:NTM], AF.Identity),
              reads=(pt,), writes=(sg,))
        S.dma("sp", c["d_vtm"], v_tm[n * 128:(n + 1) * 128, :], sg[:, 0:256], reads=(sg,))
        S.dma("sp", c["d_sm"], sm_tm[n * 128:(n + 1) * 128, :], sg[:, 256:272], reads=(sg,))


def phase_A(S, c, hT, v_tm, catT, CT_d, ST_d, mask_d, lam_d, subw_d, lam_init, T):
    ps = c["ps"]
    Kc = c["a_Kc"]
    Va = c["a_V"]
    Qall = c["a_Qall"]
    Qp = c["a_Qp"]
    ld = c["a_ld"]
    cs = c["a_cs"]
    tmp = c["a_tmp"]
    pb = c["a_P"]
    mask = c["a_mask"]
    ones32 = c["ones32"]
    lamt = c["a_lam"]
    sb64 = c["a_sb64"]
    aob = c["a_aob"]
    subw = c["a_subw"]
    xo = c["a_xo"]
    sel = c["a_sel"]
    epsT = c["epsT"]
    gc = c["gconst"]
    d_hT, d_vtm, d_cat = c["d_hT"], c["d_vtm"], c["d_cat"]
    scale = 32.0 ** -0.5
    ntile = T // 512
    sbanks = [ps[0], ps[1], c["psTf"]]

    S.dma("sp", mask, mask[:, :, :], mask_d)
    S.add("dve", CALL("memset", sel[:, :], 0.0), writes=(sel,))
    S.add("dve", CALL("memset", sel[64:65, :], 1.0), writes=(sel,))
    S.dma("sp", subw, subw[:, :], subw_d)
    S.dma("sp", lamt, lamt[:, 0:128], lam_d.partition_broadcast(128))
    for j in range(2):
        S.add("dve", CALL("tensor_tensor", lamt[:, 0:32] if j == 0 else lamt[:, 64:96],
                          lamt[:, 64 * j:64 * j + 32], lamt[:, 64 * j + 32:64 * j + 64], ALU.mult), reads=(lamt,), writes=(lamt,))
        S.add("dve", CALL("tensor_reduce", lamt[:, 130 + j:131 + j], lamt[:, 64 * j:64 * j + 32], AX.X, ALU.add), reads=(lamt,), writes=(lamt,))
    S.add("act", CALL("activation", lamt[:, 132:134], lamt[:, 130:132], AF.Exp), reads=(lamt,), writes=(lamt,))
    S.add("dve", CALL("tensor_tensor", lamt[:, 134:135], lamt[:, 133:134], lamt[:, 132:133], ALU.subtract), reads=(lamt,), writes=(lamt,))
    S.add("dve", CALL("tensor_scalar", lamt[:, 136:137], lamt[:, 134:135], -float(lam_init), 0.0, ALU.add, ALU.add), reads=(lamt,), writes=(lamt,))
    neglam = lamt[0:64, 136:137]

    ri = [0]

    def rotary(dst_ap, dst_tile, name_main, name_part, col0):
        i = ri[0]
        ri[0] += 1
        l0, l1 = ld[(2 * i) % 4], ld[(2 * i + 1) % 4]
        cst = cs[i % 2]
        t0, t1 = tmp[0], tmp[1]
        r0 = RB_OFF[name_main][0]
        r1 = RB_OFF[name_part][0]
        S.dma("sp", l0, l0[:, :], hT[r0:r0 + 128, col0:col0 + 512], reads=(d_hT,))
        S.dma("sp", l1, l1[:, :], hT[r1:r1 + 128, col0:col0 + 512], reads=(d_hT,))
        S.dma("sp", cst, cst[:, 0, :], CT_d[:, col0:col0 + 512])
        S.dma("sp", cst, cst[:, 1, :], ST_d[:, col0:col0 + 512])
        S.add("dve", CALL("tensor_tensor", t0[:, :], l0[:, :], cst[:, 0, :], ALU.mult), reads=(l0, cst), writes=(t0,))
        S.add("pool", CALL("tensor_tensor", t1[:, :], l1[:, :], cst[:, 1, :], ALU.mult), reads=(l1, cst), writes=(t1,))
        S.add("dve", CALL("tensor_tensor", dst_ap, t0[:, :], t1[:, :], ALU.add), reads=(t0, t1), writes=(dst_tile,))

    for cch in range(2):
        for kb in range(ntile):
            rotary(Kc[:, cch, kb * 512:(kb + 1) * 512], Kc, f"akm{cch}", f"akp{cch}", kb * 512)
    S.add("dve", CALL("memset", Va[:, :, 64:128], 1.0), writes=(Va,))
    pi = 0
    si = 0
    for h in range(4):
        cch = h // 2
        if h % 2 == 0:
            for t in range(ntile):
                rotary(Qall[:, t, :], Qall, f"aqm{cch}", f"aqp{cch}", t * 512)
        S.dma("pool", Va, Va[:, :, 0:64], v_tm[:, h * 64:(h + 1) * 64].rearrange("(n p) d -> p n d", p=128), reads=(d_vtm,))
        def prep_q(t):
            for m_ in range(2):
                j = (h % 2) * 2 + m_
                qp = Qp[(t % 2) * 2 + m_]
                S.add("dve", CALL("tensor_scalar", qp[:, :], Qall[:, t, :], gc[:, GC["cmask"] + j:GC["cmask"] + j + 1], 0.0, ALU.mult, ALU.add),
                      reads=(Qall, gc), writes=(qp,))

        prep_q(0)
        for t in range(ntile):
            Qpt = [Qp[(t % 2) * 2], Qp[(t % 2) * 2 + 1]]
            nk = 4 * (t + 1)
            units = [(m_, kt) for m_ in range(2) for kt in range(nk)]
            ubuf = {}

            def emit_S(u):
                nonlocal si, pi
                m_, kt = u
                sp_ = sbanks[si % 3]
                si += 1
                P = pb[pi % 4]
                pi += 1
                ubuf[u] = (sp_, P)
                S.add("pe", CALL("matmul", sp_[:, :], Kc[:, cch, kt * 128:(kt + 1) * 128], Qpt[m_][:, :], start=True, stop=True),
                      reads=(Kc, Qpt[m_]), writes=(sp_,))

            def emit_rest(u):
                m_, kt = u
                sp_, P = ubuf.pop(u)
                po = ps[2 + m_]
                S.add("act", CALL("activation", P[:, :], sp_[:, :], AF.Exp, scale=scale), reads=(sp_,), writes=(P,))
                if kt >= 4 * t:
                    r = kt - 4 * t
                    S.add("dve", CALL("tensor_tensor", P[:, :], P[:, :], mask[:, r, :], ALU.mult), reads=(P, mask), writes=(P,))
                S.add("pe", CALL("matmul", po[:, :], Va[:, kt, :], P[:, :], start=(kt == 0), stop=(kt == nk - 1)),
                      reads=(Va, P), writes=(po,))

            SK = 2
            for u in units[0:SK]:
                emit_S(u)
            for i, u in enumerate(units):
                if i + SK < len(units):
                    emit_S(units[i + SK])
                emit_rest(u)
            if t + 1 < ntile:
                prep_q(t + 1)
            rl0, on0, rl1, on1, a, asq = sb64
            for m_, (rl_, on_) in enumerate(((rl0, on0), (rl1, on1))):
                x_ = xo[m_]
                S.add("act", CALL("activation", x_[:, :], ps[2 + m_][:, :], AF.Identity), reads=(ps[2 + m_],), writes=(x_,))
                S.add("pe", CALL("matmul", ps[4 + m_][0:64, :], sel[:, :], x_[:, :], start=True, stop=True),
                      reads=(sel, x_), writes=(ps[4 + m_],))
                S.add("dve", CALL("reciprocal", rl_[:, :], ps[4 + m_][0:64, :]), reads=(ps[4 + m_],), writes=(rl_,))
                S.add("dve", CALL("tensor_tensor", on_[:, :], x_[0:64, :], rl_[:, :], ALU.mult), reads=(x_, rl_), writes=(on_,))
            S.add("dve", CALL("scalar_tensor_tensor", a[:, :], on1[:, :], neglam, on0[:, :], ALU.mult, ALU.add),
                  reads=(on1, on0, lamt), writes=(a,))
            S.add("act", CALL("activation", asq[:, :], a[:, :], AF.Square), reads=(a,), writes=(asq,))
            S.add("pe", CALL("matmul", ps[6][0:64, :], ones32[0:64, 0:64], asq[:, :], start=True, stop=True),
                  reads=(ones32, asq), writes=(ps[6],))
            S.add("act", CALL("activation", asq[:, :], ps[6][0:64, :], AF.Ln, bias=epsT[0:64, 1:2], scale=1.0 / 64),
                  reads=(ps[6], epsT), writes=(asq,))
            S.add("act", CALL("activation", asq[:, :], asq[:, :], AF.Exp, scale=-0.5), reads=(asq,), writes=(asq,))
            S.add("dve", CALL("tensor_tensor", a[:, :], a[:, :], asq[:, :], ALU.mult), reads=(a, asq), writes=(a,))
            ao = aob[t % 2]
            S.add("dve", CALL("tensor_scalar", ao[:, :], a[:, :], subw[:, 0:1], 1.0 - float(lam_init), ALU.mult, ALU.mult),
                  reads=(a, subw), writes=(ao,))
            S.dma("sp", d_cat, catT[h * 64:(h + 1) * 64, t * 512:(t + 1) * 512], ao[:, :], reads=(ao,))


def pack_mamba(P):
    cw = P["m_conv_w"]
    cb = P["m_conv_b"]
    out = {}
    out["m_cwx"] = np.ascontiguousarray(cw[:, 0:512].reshape(4, 8, 64).transpose(2, 1, 0))
    out["m_cwbc"] = np.ascontiguousarray(cw[:, 512:1024].reshape(4, 4, 128).transpose(2, 1, 0))
    out["m_cbx"] = np.ascontiguousarray(cb[0:512].reshape(8, 64).T)
    out["m_cbbc"] = np.ascontiguousarray(cb[512:1024].reshape(4, 128).T)
    out["m_dtb"] = np.ascontiguousarray(P["m_dt_bias"].reshape(1, 8))
    out["m_alog"] = np.ascontiguousarray(P["m_A_log"].reshape(1, 8))
    out["m_D"] = np.ascontiguousarray(np.repeat(P["m_D"].reshape(1, 8), 64, 0))
    out["m_nw"] = np.ascontiguousarray(P["m_norm_w"].reshape(8, 64).T)
    return out


class StopHere(Exception):
    pass


def dbg(S, c, name, tile, ap):
    f = c.get("dbgfn")
    if f is not None:
        f(S, name, tile, ap)


def chk(c, tag):
    if c.get("stop") == tag:
        raise StopHere(tag)


def tri_const():
    return np.triu(np.ones((128, 128), np.float32))


def conv_block(S, w_tile, x, xt, w_ap_fn, acc, acct, M, out_ap, out_tile, bias_ap, bias_tile):
    S.add("dve", CALL("tensor_scalar", acc[0:M, :], x[0:M, 0:512], w_ap_fn(0), 0.0, ALU.mult, ALU.add),
          reads=(xt, w_tile), writes=(acct,))
    for j in range(1, 4):
        S.add("dve", CALL("scalar_tensor_tensor", acc[0:M, :], x[0:M, j:j + 512], w_ap_fn(j), acc[0:M, :], ALU.mult, ALU.add),
              reads=(xt, acct, w_tile), writes=(acct,))
    if bias_ap is not None:
        S.add("act", CALL("activation", out_ap, acc[0:M, :], AF.Silu, bias=bias_ap), reads=(acct, bias_tile), writes=(out_tile,))
    else:
        S.add("act", CALL("activation", out_ap, acc[0:M, :], AF.Silu), reads=(acct,), writes=(out_tile,))


def load_halo(S, dst, hT, r0, M, t, d_hT):
    if t == 0:
        S.add("pool", CALL("memset", dst[0:M, 0:3], 0.0), writes=(dst,))
        S.dma("sp", dst, dst[0:M, 3:515], hT[r0:r0 + M, 0:512], reads=(d_hT,))
    else:
        S.dma("sp", dst, dst[0:M, 0:515], hT[r0:r0 + M, t * 512 - 3:t * 512 + 512], reads=(d_hT,))


def phase_M(S, c, hT, sm_tm, catT, W, T):
    ps = c["ps"]
    d_hT, d_sm, d_cat = c["d_hT"], c["d_sm"], c["d_cat"]
    NB = T // 128
    ntile = T // 512
    ones32 = c["ones32"]
    epsT = c["epsT"]
    tri = c["tri"]
    identb = c["identb"]
    cwx, cwbc, cbx, cbbc = c["m_cwx"], c["m_cwbc"], c["m_cbx"], c["m_cbbc"]
    Dt, nw = c["m_Dt"], c["m_nw"]
    sm128 = c["sm128"]
    dt, aa, acs = c["m_dt"], c["m_a"], c["m_acs"]
    bcast = c["m_bc"]
    xin = c["m_xin"]
    acc = c["m_acc"]
    xsT = c["m_xsT"]
    xsb = c["m_xsb"]
    BT, CTt = c["m_BT"], c["m_CT"]
    xs_tok = c["m_xs_tok"]
    B_tok = c["m_B_tok"]
    rhsR = c["m_rhsR"]
    ER = c["m_ER"]
    Eh = c["m_E"]
    Gm = c["m_Gm"]
    Mp = c["m_Mp"]
    Cdec = c["m_Cdec"]
    xdt = c["m_xdt"]
    xsw = c["m_xsw"]
    wv = c["m_wv"]
    H = c["m_H"]
    Hbf = c["m_Hbf"]
    ysb = c["m_ysb"]
    mzs = c["m_mz"]
    ysq = c["m_ysq"]
    rs = c["m_rs"]
    yo = c["m_yo"]

    S.dma("sp", cwx, cwx[:, :, :], W["m_cwx"])
    S.dma("sp", cwbc, cwbc[:, :, :], W["m_cwbc"])
    S.dma("sp", cbx, cbx[:, :], W["m_cbx"])
    S.dma("sp", cbbc, cbbc[:, :], W["m_cbbc"])
    S.dma("sp", Dt, Dt[:, :], W["m_D"])
    S.dma("sp", nw, nw[:, :], W["m_nw"])
    S.dma("sp", bcast, bcast[:, 0:8], W["m_dtb"].partition_broadcast(128))
    S.dma("sp", bcast, bcast[:, 8:16], W["m_alog"].partition_broadcast(128))
    S.dma("sp", sm128, sm128[:, :, :], sm_tm.rearrange("(n p) c -> p n c", p=128), reads=(d_sm,))
    S.add("act", CALL("activation", bcast[:, 8:16], bcast[:, 8:16], AF.Exp), reads=(bcast,), writes=(bcast,))
    S.add("dve", CALL("tensor_scalar", bcast[:, 8:16], bcast[:, 8:16], -1.0, 0.0, ALU.mult, ALU.add), reads=(bcast,), writes=(bcast,))
    S.add("dve", CALL("tensor_tensor", dt[:, :, :], sm128[:, :, 0:8], bcast[:, 0:8].unsqueeze(1).to_broadcast([128, NB, 8]), ALU.add),
          reads=(sm128, bcast), writes=(dt,))
    S.add("act", CALL("activation", dt[:, :, :], dt[:, :, :], AF.Exp), reads=(dt,), writes=(dt,))
    S.add("act", CALL("activation", dt[:, :, :], dt[:, :, :], AF.Ln, bias=1.0), reads=(dt,), writes=(dt,))
    S.add("dve", CALL("tensor_tensor", aa[:, :, :], dt[:, :, :], bcast[:, 8:16].unsqueeze(1).to_broadcast([128, NB, 8]), ALU.mult),
          reads=(dt, bcast), writes=(aa,))
    S.add("pe", CALL("matmul", ps[0][:, 0:NB * 8], tri[:, :], aa[:, :, :].rearrange("p n h -> p (n h)"), start=True, stop=True),
          reads=(tri, aa), writes=(ps[0],))
    S.add("dve", CALL("tensor_copy", acs[:, :, :].rearrange("p n h -> p (n h)"), ps[0][:, 0:NB * 8]), reads=(ps[0],), writes=(acs,))
    S.add("pool", CALL("memset", H[:, :, :], 0.0), writes=(H,))
    S.add("pool", CALL("memset", Hbf[:, :, :], 0.0), writes=(Hbf,))

    chk(c, "smalls")
    psR = [ps[0], ps[1]]
    psG, psY0, psY1, psH, psS = ps[2], ps[3], ps[4], ps[5], ps[6]
    psTb = c["psTb"]

    def Rap(h):
        return psR[h // 4][:, (h % 4) * 128:(h % 4 + 1) * 128]

    for t in range(ntile):
        for h in range(8):
            xi, ac = xin[h % 2], acc[h % 2]
            load_halo(S, xi, hT, RB_OFF[f"mx{h}"][0], 64, t, d_hT)
            conv_block(S, cwx, xi, xi, lambda j, h=h: cwx[:, h, j:j + 1], ac, ac, 64, xsT[:, h, :], xsT, cbx[:, h:h + 1], cbx)
        S.add("pool", CALL("tensor_copy", xsb[:, :, :], xsT[:, :, :]), reads=(xsT,), writes=(xsb,))
        for q in range(4):
            xi, ac = xin[q % 2], acc[q % 2]
            nm = ("mB0", "mB1", "mC0", "mC1")[q]
            dstt = BT if q < 2 else CTt
            load_halo(S, xi, hT, RB_OFF[nm][0], 128, t, d_hT)
            conv_block(S, cwbc, xi, xi, lambda j, q=q: cwbc[:, q, j:j + 1], ac, ac, 128, dstt[:, q % 2, :], dstt, cbbc[:, q:q + 1], cbbc)
        if t == 0:
            dbg(S, c, "xsT", xsT, xsT[:, :, :])
            dbg(S, c, "dt", dt, dt[:, :, :])
            dbg(S, c, "acs", acs, acs[:, :, :])
        chk(c, "conv")
        for nb in range(c.get("nbr", 4)):
            for h in range(8):
                S.add("pe", CALL("transpose", psTb[:, h * 64:(h + 1) * 64], xsb[:, h, nb * 128:(nb + 1) * 128], identb[0:64, 0:64]),
                      reads=(xsb, identb), writes=(psTb,))
            var = c.get("var", "")
            if "noB" not in var:
                for g in range(2):
                    S.add("pe", CALL("transpose", psTb[:, 512 + g * 128:512 + (g + 1) * 128], BT[:, g, nb * 128:(nb + 1) * 128], identb[:, :]),
                          reads=(BT, identb), writes=(psTb,))
            if True:
                S.add("dve", CALL("tensor_copy", xs_tok[:, nb, :], psTb[:, 0:512]), reads=(psTb,), writes=(xs_tok,))
            else:
                S.add("act", CALL("activation", xs_tok[:, nb, :], psTb[:, 0:512], AF.Identity), reads=(psTb,), writes=(xs_tok,))
            if "noB" not in var:
                S.add("dve", CALL("tensor_copy", B_tok[:, nb, :], psTb[:, 512:768]), reads=(psTb,), writes=(B_tok,))
        chk(c, "transp")
        for nb in range(4):
            n = t * 4 + nb
            cols = slice(nb * 128, (nb + 1) * 128)
            for h in range(8):
                eng = "pool"
                S.add(eng, CALL("tensor_scalar", rhsR[:, h, :], tri[:, :], aa[:, n, h:h + 1], 0.0, ALU.mult, ALU.add),
                      reads=(tri, aa), writes=(rhsR,))
            for hh in range(2):
                S.add("pe", CALL("matmul", psR[hh][:, :], ones32[:, :], rhsR[:, hh * 4:(hh + 1) * 4, :].rearrange("p h l -> p (h l)"),
                                                       start=True, stop=True), reads=(ones32, rhsR), writes=(psR[hh],))
            for hh in range(2):
                S.add("act", CALL("activation", ER[:, hh * 4:(hh + 1) * 4, :].rearrange("p h l -> p (h l)"), psR[hh][:, :], AF.Exp),
                      reads=(psR[hh],), writes=(ER,))
            for h in range(8):
                S.add("dve", CALL("tensor_scalar", Eh[:, h, :], Rap(h), acs[:, n, h:h + 1], 0.0, ALU.subtract, ALU.min),
                      reads=(psR[h // 4], acs), writes=(Eh,))
            S.add("act", CALL("activation", Eh[:, :, :], Eh[:, :, :], AF.Exp), reads=(Eh,), writes=(Eh,))
            chk(c, "RE")
            for g in range(2):
                S.add("pe", CALL("matmul", psG[:, g * 128:(g + 1) * 128], BT[:, g, cols], CTt[:, g, cols], start=True, stop=True),
                      reads=(BT, CTt), writes=(psG,))
            S.add("dve", CALL("tensor_tensor", Gm[:, :, :], psG[:, 0:256].rearrange("p (g l) -> p g l", g=2),
                                                    tri[:, :].unsqueeze(1).to_broadcast([128, 2, 128]), ALU.mult),
                  reads=(psG, tri), writes=(Gm,))
            for g in range(2):
                S.add("dve", CALL("tensor_tensor", Mp[:, g * 4:(g + 1) * 4, :], Eh[:, g * 4:(g + 1) * 4, :],
                                                              Gm[:, g, :].unsqueeze(1).to_broadcast([128, 4, 128]), ALU.mult),
                      reads=(Eh, Gm), writes=(Mp,))
                S.add("dve", CALL("tensor_tensor", Cdec[:, g * 4:(g + 1) * 4, :], ER[:, g * 4:(g + 1) * 4, :],
                                                              CTt[:, g, cols].unsqueeze(1).to_broadcast([128, 4, 128]), ALU.mult),
                      reads=(ER, CTt), writes=(Cdec,))
            S.add("dve", CALL("tensor_tensor", xdt[:, :, :], xs_tok[:, nb, :].rearrange("p (h d) -> p h d", h=8),
                                                    dt[:, n, :].unsqueeze(2).to_broadcast([128, 8, 64]), ALU.mult),
                  reads=(xs_tok, dt), writes=(xdt,))
            for h in range(8):
                py = psY0 if h < 4 else psY1
                oc = slice((h % 4) * 128, (h % 4 + 1) * 128)
                S.add("pe", CALL("matmul", py[0:64, oc], xdt[:, h, :], Mp[:, h, :], start=True, stop=False),
                      reads=(xdt, Mp), writes=(py,))
                S.add("pe", CALL("matmul", py[0:64, oc], Hbf[:, h, :], Cdec[:, h, :], start=False, stop=True),
                      reads=(Hbf, Cdec), writes=(py,))
            S.add("act", CALL("activation", ysb[:, 0:4, cols], psY0[0:64, :].rearrange("p (h l) -> p h l", h=4), AF.Identity),
                  reads=(psY0,), writes=(ysb,))
            S.add("act", CALL("activation", ysb[:, 4:8, cols], psY1[0:64, :].rearrange("p (h l) -> p h l", h=4), AF.Identity),
                  reads=(psY1,), writes=(ysb,))
            if n == 0:
                dbg(S, c, "ER", ER, ER[:, :, :])
                dbg(S, c, "Eh", Eh, Eh[:, :, :])
                dbg(S, c, "Gm", Gm, Gm[:, :, :])
                dbg(S, c, "xs_tok", xs_tok, xs_tok[:, 0, :])
            if n == 1:
                dbg(S, c, "H1", H, H[:, :, :])
            chk(c, "Y")
            for hh in range(2):
                S.add("dve", CALL("tensor_tensor", wv[:, hh * 4:(hh + 1) * 4],
                                                               psR[hh][:, :].rearrange("p (h l) -> p h l", h=4)[:, :, 127],
                                                               acs[:, n, hh * 4:(hh + 1) * 4], ALU.subtract),
                      reads=(psR[hh], acs), writes=(wv,))
            S.add("act", CALL("activation", wv[:, :], wv[:, :], AF.Exp), reads=(wv,), writes=(wv,))
            S.add("dve", CALL("tensor_tensor", wv[:, :], wv[:, :], dt[:, n, :], ALU.mult), reads=(wv, dt), writes=(wv,))
            S.add("dve", CALL("tensor_tensor", xsw[:, :, :], xs_tok[:, nb, :].rearrange("p (h d) -> p h d", h=8),
                                                    wv[:, :].unsqueeze(2).to_broadcast([128, 8, 64]), ALU.mult),
                  reads=(xs_tok, wv), writes=(xsw,))
            for g in range(2):
                S.add("pe", CALL("matmul", psH[:, g * 256:(g + 1) * 256], B_tok[:, nb, g * 128:(g + 1) * 128],
                                                     xsw[:, g * 4:(g + 1) * 4, :].rearrange("p h d -> p (h d)"), start=True, stop=True),
                      reads=(B_tok, xsw), writes=(psH,))
            S.add("dve", CALL("tensor_tensor", H[:, :, :], H[:, :, :], ER[:, :, 127:128].to_broadcast([128, 8, 64]), ALU.mult),
                  reads=(H, ER), writes=(H,))
            S.add("dve", CALL("tensor_tensor", H[:, :, :], H[:, :, :], psH[:, :].rearrange("p (h d) -> p h d", h=8), ALU.add),
                  reads=(H, psH), writes=(H,))
            S.add("act", CALL("activation", Hbf[:, :, :], H[:, :, :], AF.Identity), reads=(H,), writes=(Hbf,))
        if t == 0:
            dbg(S, c, "ysb", ysb, ysb[:, :, :])
        chk(c, "state")
        for h in range(8):
            S.add("dve", CALL("scalar_tensor_tensor", ysb[:, h, :], xsT[:, h, :], Dt[:, h:h + 1], ysb[:, h, :], ALU.mult, ALU.add),
                  reads=(xsT, Dt, ysb), writes=(ysb,))
            mz = mzs[h % 2]
            r0 = RB_OFF[f"mz{h}"][0]
            S.dma("sp", mz, mz[:, :], hT[r0:r0 + 64, t * 512:(t + 1) * 512], reads=(d_hT,))
            S.add("act", CALL("activation", mz[:, :], mz[:, :], AF.Silu), reads=(mz,), writes=(mz,))
            S.add("pool", CALL("tensor_tensor", ysb[:, h, :], ysb[:, h, :], mz[:, :], ALU.mult), reads=(ysb, mz), writes=(ysb,))
        for g in range(2):
            for hh in range(4):
                h = g * 4 + hh
                yq = ysq[hh % 2]
                S.add("act", CALL("activation", yq[:, :], ysb[:, h, :], AF.Square), reads=(ysb,), writes=(yq,))
                S.add("pe", CALL("matmul", psS[0:64, :], ones32[0:64, 0:64], yq[:, :], start=(hh == 0), stop=(hh == 3)),
                      reads=(ones32, yq), writes=(psS,))
            S.add("act", CALL("activation", rs[:, :], psS[0:64, :], AF.Ln, bias=epsT[0:64, 1:2], scale=1.0 / 256), reads=(psS, epsT), writes=(rs,))
            S.add("act", CALL("activation", rs[:, :], rs[:, :], AF.Exp, scale=-0.5), reads=(rs,), writes=(rs,))
            for hh in range(4):
                h = g * 4 + hh
                yt = yo[hh % 2]
                S.add("dve", CALL("scalar_tensor_tensor", yt[:, :], ysb[:, h, :], nw[:, h:h + 1], rs[:, :], ALU.mult, ALU.mult),
                      reads=(ysb, nw, rs), writes=(yt,))
                S.dma("sp", d_cat, catT[256 + h * 64:256 + (h + 1) * 64, t * 512:(t + 1) * 512], yt[:, :], reads=(yt,))


GC = dict(tri=0, triBD=128, blkones=256, ident=384, mbU=512, mbL=640, cmask=768, mb3=772)
GCW = 804


def g_consts():
    s = np.arange(128)[:, None]
    l = np.arange(128)[None, :]
    same = (s // 32) == (l // 32)
    out = np.zeros((128, GCW), np.float32)
    out[:, 0:128] = (s <= l)
    out[:, 128:256] = (s <= l) & same
    out[:, 256:384] = same
    out[:, 384:512] = np.eye(128)
    out[:, 512:640] = np.where((s < l) & same, 0.0, -30000.0)
    out[:, 640:768] = np.where((l < s) & same, 0.0, 30000.0)
    out[:, 768:772] = (np.arange(128)[:, None] // 32) == np.arange(4)[None, :]
    a = np.arange(32)
    out[0:32, 772:804] = np.where(a[:, None] <= a[None, :], 0.0, -30000.0)
    return out


def pack_gdn(P):
    cw = P["g_conv_w"]
    out = {}
    out["g_cw"] = np.ascontiguousarray(cw.reshape(4, 12, 64).transpose(2, 1, 0))
    out["g_dtb"] = np.ascontiguousarray(P["g_dt_bias"].reshape(1, 4))
    out["g_alog"] = np.ascontiguousarray(P["g_A_log"].reshape(1, 4))
    out["g_nw"] = np.ascontiguousarray(P["g_norm_w"].reshape(64, 1))
    return out


def phase_G(S, c, hT, sm_tm, catT, W, gsm_d, T):
    ps = c["ps"]
    psTb = c["psTb"]
    d_hT, d_sm, d_cat = c["d_hT"], c["d_sm"], c["d_cat"]
    d_gsm = c["d_gsm"]
    NB = T // 128
    NC = T // 32
    ntile = T // 512
    ones32, epsT, identb = c["ones32"], c["epsT"], c["identb"]
    gc = c["gconst"]
    def G(name, w=128, rows=128):
        o = GC[name]
        return gc[0:rows, o:o + w]
    gcw, gnw = c["g_cw"], c["g_nw"]
    sm128 = c["sm128"]
    bc = c["g_bc"]
    gg, beta, lnb, gcs, c1, glast, negb, c2 = (c[k] for k in ("g_g", "g_beta", "g_lnb", "g_gcs", "g_c1", "g_glast", "g_negb", "g_c2"))
    gstage = c["g_stage"]
    sm32 = c["g_sm32"]
    dcb = c["g_dcb"]
    rhs16 = c["g_rhs16"]
    xin, acc = c["m_xin"], c["m_acc"]
    qT, kT, vT = c["g_qT"], c["g_kT"], c["g_vT"]
    qTb, kTb, vTb = c["g_qTb"], c["g_kTb"], c["g_vTb"]
    sq = c["g_sq"]
    rhsR = c["g_rhsR"]
    rhsR2 = c["g_rhsR2"]
    E1, E2 = c["g_E1"], c["g_E2"]
    X = c["g_X"]
    XT = c["g_XT"]
    TTf = c["g_TT"]
    Tb = c["g_Tb"]
    ktok, vtok = c["g_ktok"], c["g_vtok"]
    kbg, vb = c["g_kbg"], c["g_vb"]
    wTb = c["g_wTb"]
    ERg = c["g_ERg"]
    qd = c["g_qd"]
    a3 = c["g_a3"]
    attc = c["g_attc"]
    kdec = c["g_kdec"]
    usb = c["g_usb"]
    vnb = c["g_vnb"]
    St = c["g_S"]
    Sbf = c["g_Sbf"]
    oT = c["g_oT"]
    rs = c["g_rs"]
    go = c["g_go"]

    S.dma("sp", gcw, gcw[:, :, :], W["g_cw"])
    S.dma("sp", gnw, gnw[:, :], W["g_nw"])
    S.dma("sp", bc, bc[:, 0:4], W["g_dtb"].partition_broadcast(128))
    S.dma("sp", bc, bc[:, 4:8], W["g_alog"].partition_broadcast(128))
    S.dma("sp", sm128, sm128[:, :, :], sm_tm.rearrange("(n p) c -> p n c", p=128), reads=(d_sm,))
    S.add("act", CALL("activation", bc[:, 4:8], bc[:, 4:8], AF.Exp), reads=(bc,), writes=(bc,))
    S.add("dve", CALL("tensor_scalar", bc[:, 4:8], bc[:, 4:8], -1.0, 0.0, ALU.mult, ALU.add), reads=(bc,), writes=(bc,))
    S.add("act", CALL("activation", beta[:, :, :], sm128[:, :, 8:12], AF.Exp, scale=-1.0), reads=(sm128,), writes=(beta,))
    S.add("act", CALL("activation", lnb[:, :, :], beta[:, :, :], AF.Ln, bias=1.0), reads=(beta,), writes=(lnb,))
    S.add("dve", CALL("tensor_scalar", lnb[:, :, :], lnb[:, :, :], -1.0, 0.0, ALU.mult, ALU.add), reads=(lnb,), writes=(lnb,))
    S.add("dve", CALL("tensor_scalar", beta[:, :, :], beta[:, :, :], 1.0, 0.0, ALU.add, ALU.add), reads=(beta,), writes=(beta,))
    S.add("dve", CALL("reciprocal", beta[:, :, :], beta[:, :, :]), reads=(beta,), writes=(beta,))
    S.add("dve", CALL("tensor_scalar", negb[:, :, :], beta[:, :, :], -1.0, 0.0, ALU.mult, ALU.add), reads=(beta,), writes=(negb,))
    S.add("dve", CALL("tensor_tensor", gg[:, :, :], sm128[:, :, 12:16], bc[:, 0:4].unsqueeze(1).to_broadcast([128, NB, 4]), ALU.add),
          reads=(sm128, bc), writes=(gg,))
    S.add("act", CALL("activation", gg[:, :, :], gg[:, :, :], AF.Exp), reads=(gg,), writes=(gg,))
    S.add("act", CALL("activation", gg[:, :, :], gg[:, :, :], AF.Ln, bias=1.0), reads=(gg,), writes=(gg,))
    S.add("dve", CALL("tensor_tensor", gg[:, :, :], gg[:, :, :], bc[:, 4:8].unsqueeze(1).to_broadcast([128, NB, 4]), ALU.mult),
          reads=(gg, bc), writes=(gg,))
    S.add("pe", CALL("matmul", ps[0][:, 0:NB * 4], G("triBD"), gg[:, :, :].rearrange("p n h -> p (n h)"), start=True, stop=True),
          reads=(gc, gg), writes=(ps[0],))
    S.add("pe", CALL("matmul", ps[1][:, 0:NB * 4], G("blkones"), gg[:, :, :].rearrange("p n h -> p (n h)"), start=True, stop=True),
          reads=(gc, gg), writes=(ps[1],))
    S.add("dve", CALL("tensor_copy", gcs[:, :, :].rearrange("p n h -> p (n h)"), ps[0][:, 0:NB * 4]), reads=(ps[0],), writes=(gcs,))
    S.add("dve", CALL("tensor_copy", glast[:, :, :].rearrange("p n h -> p (n h)"), ps[1][:, 0:NB * 4]), reads=(ps[1],), writes=(glast,))
    S.add("act", CALL("activation", c1[:, :, :], gcs[:, :, :], AF.Exp), reads=(gcs,), writes=(c1,))
    S.add("dve", CALL("tensor_tensor", c1[:, :, :], c1[:, :, :], beta[:, :, :], ALU.mult), reads=(c1, beta), writes=(c1,))
    S.add("dve", CALL("tensor_tensor", c2[:, :, :], glast[:, :, :], gcs[:, :, :], ALU.subtract), reads=(glast, gcs), writes=(c2,))
    S.add("act", CALL("activation", c2[:, :, :], c2[:, :, :], AF.Exp), reads=(c2,), writes=(c2,))
    S.add("dve", CALL("tensor_copy", gstage[:, :, 0:4], gcs[:, :, :]), reads=(gcs,), writes=(gstage,))
    S.add("dve", CALL("tensor_copy", gstage[:, :, 4:8], c2[:, :, :]), reads=(c2,), writes=(gstage,))
    S.dma("sp", d_gsm, gsm_d.rearrange("(n p) c -> p n c", p=128), gstage[:, :, :], reads=(gstage,))
    S.dma("sp", sm32, sm32[:, :, :], gsm_d.rearrange("(n p) c -> p n c", p=32), reads=(d_gsm,))
    S.add("dve", CALL("tensor_tensor", rhs16[:, :, :, :], gg[:, :, :].unsqueeze(2).to_broadcast([128, NB, 4, 4]),
                                            G("cmask", 4).unsqueeze(1).unsqueeze(3).to_broadcast([128, NB, 4, 4]), ALU.mult),
          reads=(gg, gc), writes=(rhs16,))
    for q0 in range(0, NB * 16, 512):
        q1 = min(NB * 16, q0 + 512)
        S.add("pe", CALL("matmul", ps[2][:, 0:q1 - q0], ones32[:, :], rhs16[:, :, :, :].rearrange("p n c h -> p (n c h)")[:, q0:q1],
                                                     start=True, stop=True), reads=(ones32, rhs16), writes=(ps[2],))
        S.add("act", CALL("activation", dcb[:, :, :, :].rearrange("p n c h -> p (n c h)")[:, q0:q1], ps[2][:, 0:q1 - q0], AF.Exp),
              reads=(ps[2],), writes=(dcb,))
    S.add("pool", CALL("memset", St[:, :, :], 0.0), writes=(St,))
    S.add("pool", CALL("memset", Sbf[:, :, :], 0.0), writes=(Sbf,))
    chk(c, "gsmalls")

    psRg, psRgb, psKK, psA, psAT, psT, psX = ps[0], ps[1], ps[2], ps[3], ps[4], ps[5], ps[6]
    psW, psO, psSS, psAt, pswT = ps[1], ps[2], ps[3], ps[4], ps[5]

    def h4(t):
        return t[:, :].rearrange("p (h l) -> p h l", h=4)

    for t in range(ntile):
        for j in range(12):
            xi, ac = xin[j % 2], acc[j % 2]
            nm = ("gq", "gk", "gv")[j // 4] + str(j % 4)
            dst = (qT, kT, vT)[j // 4]
            load_halo(S, xi, hT, RB_OFF[nm][0], 64, t, d_hT)
            conv_block(S, gcw, xi, xi, lambda jj, j=j: gcw[:, j, jj:jj + 1], ac, ac, 64, dst[:, j % 4, :], dst, None, None)
        for qi, (src, dstb, scl) in enumerate(((qT, qTb, 0.125), (kT, kTb, 1.0))):
            for h in range(4):
                s_ = sq[h % 2]
                S.add("act", CALL("activation", s_[:, :], src[:, h, :], AF.Square), reads=(src,), writes=(s_,))
                S.add("pe", CALL("matmul", psX[0:64, :], ones32[0:64, 0:64], s_[:, :], start=True, stop=True),
                      reads=(ones32, s_), writes=(psX,))
                S.add("act", CALL("activation", s_[:, :], psX[0:64, :], AF.Ln, bias=epsT[0:64, 2:3]), reads=(psX, epsT), writes=(s_,))
                S.add("act", CALL("activation", s_[:, :], s_[:, :], AF.Exp, scale=-0.5), reads=(s_,), writes=(s_,))
                S.add("dve", CALL("scalar_tensor_tensor",
                    dstb[:, h, :], src[:, h, :], scl, s_[:, :], ALU.mult, ALU.mult), reads=(src, s_), writes=(dstb,))
        S.add("pool", CALL("tensor_copy", vTb[:, :, :], vT[:, :, :]), reads=(vT,), writes=(vTb,))
        chk(c, "gconv")
        for nb in range(4):
            n = t * 4 + nb
            cols = slice(nb * 128, (nb + 1) * 128)
            for h in range(4):
                S.add("pool", CALL("tensor_scalar", rhsR[:, h, :], G("triBD"), gg[:, n, h:h + 1], 0.0, ALU.mult, ALU.add),
                      reads=(gc, gg), writes=(rhsR,))
                S.add("dve", CALL("scalar_tensor_tensor", rhsR2[:, h, :], G("ident"), lnb[:, n, h:h + 1], rhsR[:, h, :], ALU.mult, ALU.add),
                      reads=(gc, lnb, rhsR), writes=(rhsR2,))
            S.add("pe", CALL("matmul", psRg[:, :], ones32[:, :], rhsR[:, :, :].rearrange("p h l -> p (h l)"), start=True, stop=True),
                  reads=(ones32, rhsR), writes=(psRg,))
            S.add("pe", CALL("matmul", psRgb[:, :], ones32[:, :], rhsR2[:, :, :].rearrange("p h l -> p (h l)"), start=True, stop=True),
                  reads=(ones32, rhsR2), writes=(psRgb,))
            for h in range(4):
                S.add("pe", CALL("matmul", psKK[:, h * 128:(h + 1) * 128], kTb[:, h, cols], kTb[:, h, cols], start=True, stop=True),
                      reads=(kTb,), writes=(psKK,))
            for h in range(4):
                S.add("dve", CALL("scalar_tensor_tensor", E2[:, h, :], psRgb[:, h * 128:(h + 1) * 128], gcs[:, n, h:h + 1], G("mbU"),
                                                                    ALU.subtract, ALU.add), reads=(psRgb, gcs, gc), writes=(E2,))
                S.add("dve", CALL("scalar_tensor_tensor", E1[:, h, :], psRg[:, h * 128:(h + 1) * 128], gcs[:, n, h:h + 1], G("mbL"),
                                                                    ALU.subtract, ALU.add), reads=(psRg, gcs, gc), writes=(E1,))
            S.add("act", CALL("activation", E2[:, :, :], E2[:, :, :], AF.Exp), reads=(E2,), writes=(E2,))
            S.add("act", CALL("activation", E1[:, :, :], E1[:, :, :], AF.Exp, scale=-1.0), reads=(E1,), writes=(E1,))
            x0, xt0 = X[0], XT[0]
            S.add("dve", CALL("scalar_tensor_tensor", xt0[:, :, :], h4(psKK), -1.0, E2[:, :, :], ALU.mult, ALU.mult),
                  reads=(psKK, E2), writes=(xt0,))
            for h in range(4):
                S.add("dve", CALL("scalar_tensor_tensor", x0[:, h, :], psKK[:, h * 128:(h + 1) * 128], negb[:, n, h:h + 1], E1[:, h, :],
                                                                    ALU.mult, ALU.mult), reads=(psKK, negb, E1), writes=(x0,))
            S.add("pool", CALL("tensor_tensor", TTf[:, :, :], xt0[:, :, :], G("ident").unsqueeze(1).to_broadcast([128, 4, 128]), ALU.add),
                  reads=(xt0, gc), writes=(TTf,))
            for j in range(4):
                xa, xta = X[j % 2], XT[j % 2]
                xb, xtb = X[(j + 1) % 2], XT[(j + 1) % 2]
                for h in range(4):
                    S.add("pe", CALL("matmul", psA[:, h * 128:(h + 1) * 128], xta[:, h, :], xa[:, h, :], start=True, stop=True),
                          reads=(xa, xta), writes=(psA,))
                for h in range(4):
                    S.add("pe", CALL("matmul", psAT[:, h * 128:(h + 1) * 128], xa[:, h, :], xta[:, h, :], start=True, stop=True),
                          reads=(xa, xta), writes=(psAT,))
                S.add("act", CALL("activation", xb[:, :, :], h4(psA), AF.Identity), reads=(psA,), writes=(xb,))
                S.add("dve", CALL("tensor_copy", xtb[:, :, :], h4(psAT)), reads=(psAT,), writes=(xtb,))
                for h in range(4):
                    S.add("pe", CALL("matmul", psT[:, h * 128:(h + 1) * 128], xb[:, h, :], TTf[:, h, :], start=True, stop=True),
                          reads=(xb, TTf), writes=(psT,))
                S.add("dve", CALL("tensor_tensor", TTf[:, :, :], TTf[:, :, :], h4(psT), ALU.add), reads=(TTf, psT), writes=(TTf,))
            S.add("act", CALL("activation", Tb[:, :, :], TTf[:, :, :], AF.Identity), reads=(TTf,), writes=(Tb,))
            chk(c, "gsolve")
            for h in range(4):
                S.add("pe", CALL("transpose", psTb[:, h * 64:(h + 1) * 64], kTb[:, h, cols], identb[0:64, 0:64]), reads=(kTb, identb), writes=(psTb,))
            for h in range(4):
                S.add("pe", CALL("transpose", psTb[:, 256 + h * 64:256 + (h + 1) * 64], vTb[:, h, cols], identb[0:64, 0:64]), reads=(vTb, identb), writes=(psTb,))
            S.add("dve", CALL("tensor_tensor", kbg[:, :, :], psTb[:, 0:256].rearrange("p (h d) -> p h d", h=4),
                                                    c1[:, n, :].unsqueeze(2).to_broadcast([128, 4, 64]), ALU.mult), reads=(psTb, c1), writes=(kbg,))
            S.add("dve", CALL("tensor_tensor", vb[:, :, :], psTb[:, 256:512].rearrange("p (h d) -> p h d", h=4),
                                                    beta[:, n, :].unsqueeze(2).to_broadcast([128, 4, 64]), ALU.mult), reads=(psTb, beta), writes=(vb,))
            for h in range(4):
                S.add("pe", CALL("matmul", pswT[0:64, h * 128:(h + 1) * 128], kbg[:, h, :], Tb[:, h, :], start=True, stop=True),
                      reads=(kbg, Tb), writes=(pswT,))
            S.add("act", CALL("activation", wTb[:, :, :], pswT[0:64, :].rearrange("p (h l) -> p h l", h=4), AF.Identity), reads=(pswT,), writes=(wTb,))
            S.add("act", CALL("activation", ERg[:, :, :], psRg[0:64, :].rearrange("p (h l) -> p h l", h=4), AF.Exp), reads=(psRg,), writes=(ERg,))
            S.add("dve", CALL("tensor_tensor", qd[:, :, :], qTb[:, :, cols], ERg[:, :, :], ALU.mult), reads=(qTb, ERg), writes=(qd,))
            chk(c, "gprep")
            for cc in range(4):
                ch = n * 4 + cc
                cs_ = slice(nb * 128 + cc * 32, nb * 128 + cc * 32 + 32)
                cb_ = slice(cc * 32, cc * 32 + 32)
                for h in range(4):
                    S.add("pe", CALL("matmul", psAt[0:32, h * 32:(h + 1) * 32], kTb[:, h, cs_], qTb[:, h, cs_], start=True, stop=True),
                          reads=(kTb, qTb), writes=(psAt,))
                S.add("dve", CALL("tensor_tensor", a3[:, :, :], psRg[0:32, :].rearrange("p (h l) -> p h l", h=4)[:, :, cb_],
                                                        sm32[:, ch, 0:4].unsqueeze(2).to_broadcast([32, 4, 32]), ALU.subtract),
                      reads=(psRg, sm32), writes=(a3,))
                S.add("pool", CALL("tensor_tensor", a3[:, :, :], a3[:, :, :], G("mb3", 32, 32).unsqueeze(1).to_broadcast([32, 4, 32]), ALU.add),
                      reads=(a3, gc), writes=(a3,))
                S.add("act", CALL("activation", a3[:, :, :], a3[:, :, :], AF.Exp), reads=(a3,), writes=(a3,))
                S.add("dve", CALL("tensor_tensor", attc[:, :, :], psAt[0:32, 0:128].rearrange("p (h l) -> p h l", h=4), a3[:, :, :], ALU.mult),
                      reads=(psAt, a3), writes=(attc,))
                for h in range(4):
                    S.add("pe", CALL("transpose", psTb[0:32, 512 + h * 64:512 + (h + 1) * 64], kTb[:, h, cs_], identb[0:64, 0:64]),
                          reads=(kTb, identb), writes=(psTb,))
                S.add("dve", CALL("tensor_tensor", kdec[:, :, :], psTb[0:32, 512:768].rearrange("p (h d) -> p h d", h=4),
                                                        sm32[:, ch, 4:8].unsqueeze(2).to_broadcast([32, 4, 64]), ALU.mult),
                      reads=(psTb, sm32), writes=(kdec,))
                for h in range(4):
                    S.add("pe", CALL("matmul", psW[0:32, 256 + h * 64:256 + (h + 1) * 64], Tb[:, h, cb_], vb[:, h, :], start=True, stop=True),
                          reads=(Tb, vb), writes=(psW,))
                S.add("dve", CALL("tensor_copy", usb[:, :, :], psW[0:32, 256:512].rearrange("p (h d) -> p h d", h=4)), reads=(psW,), writes=(usb,))
                for h in range(4):
                    S.add("pe", CALL("matmul", psW[0:32, h * 64:(h + 1) * 64], wTb[:, h, cb_], Sbf[:, h, :], start=True, stop=True),
                          reads=(wTb, Sbf), writes=(psW,))
                S.add("dve", CALL("tensor_tensor", vnb[:, :, :], usb[:, :, :], psW[0:32, 0:256].rearrange("p (h d) -> p h d", h=4), ALU.subtract),
                      reads=(usb, psW), writes=(vnb,))
                for h in range(4):
                    oc = slice(h * 128 + cc * 32, h * 128 + cc * 32 + 32)
                    S.add("pe", CALL("matmul", psO[0:64, oc], Sbf[:, h, :], qd[:, h, cb_], start=True, stop=False),
                          reads=(Sbf, qd), writes=(psO,))
                    S.add("pe", CALL("matmul", psO[0:64, oc], vnb[:, h, :], attc[:, h, :], start=False, stop=True),
                          reads=(vnb, attc), writes=(psO,))
                for h in range(4):
                    S.add("pe", CALL("matmul", psSS[0:64, h * 64:(h + 1) * 64], kdec[:, h, :], vnb[:, h, :], start=True, stop=True),
                          reads=(kdec, vnb), writes=(psSS,))
                S.add("dve", CALL("tensor_tensor", St[:, :, :], St[:, :, :], dcb[0:64, n, cc, :].unsqueeze(2).to_broadcast([64, 4, 64]), ALU.mult),
                      reads=(St, dcb), writes=(St,))
                S.add("dve", CALL("tensor_tensor", St[:, :, :], St[:, :, :], psSS[0:64, 0:256].rearrange("p (h d) -> p h d", h=4), ALU.add),
                      reads=(St, psSS), writes=(St,))
                S.add("act", CALL("activation", Sbf[:, :, :], St[:, :, :], AF.Identity), reads=(St,), writes=(Sbf,))
            S.add("act", CALL("activation", oT[:, :, cols], psO[0:64, :].rearrange("p (h l) -> p h l", h=4), AF.Identity), reads=(psO,), writes=(oT,))
            chk(c, "gscan")
        gzt = c["g_gz4"]
        for h in range(4):
            r0 = RB_OFF[f"gz{h}"][0]
            S.dma("sp", gzt, gzt[:, h, :], hT[r0:r0 + 64, t * 512:(t + 1) * 512], reads=(d_hT,))
        S.add("act", CALL("activation", gzt[:, :, :], gzt[:, :, :], AF.Silu), reads=(gzt,), writes=(gzt,))
        for h in range(4):
            s_ = sq[h % 2]
            S.add("act", CALL("activation", s_[:, :], oT[:, h, :], AF.Square), reads=(oT,), writes=(s_,))
            S.add("pe", CALL("matmul", psX[0:64, :], ones32[0:64, 0:64], s_[:, :], start=True, stop=True), reads=(ones32, s_), writes=(psX,))
            S.add("act", CALL("activation", s_[:, :], psX[0:64, :], AF.Ln, bias=epsT[0:64, 1:2], scale=1.0 / 64), reads=(psX, epsT), writes=(s_,))
            S.add("act", CALL("activation", s_[:, :], s_[:, :], AF.Exp, scale=-0.5), reads=(s_,), writes=(s_,))
            S.add("dve", CALL("tensor_tensor", s_[:, :], s_[:, :], gzt[:, h, :], ALU.mult), reads=(s_, gzt), writes=(s_,))
            gt = go[h % 2]
            S.add("dve", CALL("scalar_tensor_tensor", gt[:, :], oT[:, h, :], gnw[:, 0:1], s_[:, :], ALU.mult, ALU.mult),
                  reads=(oT, gnw, s_), writes=(gt,))
            S.dma("sp", d_cat, catT[768 + h * 64:768 + (h + 1) * 64, t * 512:(t + 1) * 512], gt[:, :], reads=(gt,))


FF = 2816
NBLK = 11
DEPTH = 4


def ln_tile(S, c, x32, hs, gb, ecol):
    ones, ps1, ps2 = c["ones32"], c["ps"][5], c["ps"][6]
    zsq = c["f_zsq"]
    mean, rstd, msq = c["f_mean"], c["f_rstd"], c["f_msq"]
    epsT = c["epsT"]
    for k in range(8):
        zq = zsq[k % 2]
        S.add("act", CALL("activation", zq[:, :], x32[:, k, hs], AF.Square), reads=(x32,), writes=(zq,))
        S.add("pe", CALL("matmul", ps1[:, :], ones[:, :], x32[:, k, hs], start=(k == 0), stop=(k == 7)), reads=(x32, ones), writes=(ps1,))
        S.add("pe", CALL("matmul", ps2[:, :], ones[:, :], zq[:, :], start=(k == 0), stop=(k == 7)), reads=(zq, ones), writes=(ps2,))
    S.add("dve", CALL("tensor_scalar", mean[:, :], ps1[:, :], 1.0 / 1024, 0.0, ALU.mult, ALU.add), reads=(ps1,), writes=(mean,))
    S.add("dve", CALL("tensor_tensor", msq[:, :], mean[:, :], mean[:, :], ALU.mult), reads=(mean,), writes=(msq,))
    S.add("dve", CALL("scalar_tensor_tensor", rstd[:, :], ps2[:, :], 1.0 / 1024, msq[:, :], ALU.mult, ALU.subtract), reads=(ps2, msq), writes=(rstd,))
    S.add("act", CALL("activation", rstd[:, :], rstd[:, :], AF.Ln, bias=epsT[:, ecol:ecol + 1]), reads=(rstd, epsT), writes=(rstd,))
    S.add("act", CALL("activation", rstd[:, :], rstd[:, :], AF.Exp, scale=-0.5), reads=(rstd,), writes=(rstd,))
    for k in range(8):
        eng = "pool" if k % 4 == 3 else "dve"
        S.add(eng, CALL("tensor_tensor", x32[:, k, hs], x32[:, k, hs], mean[:, :], ALU.subtract), reads=(x32, mean), writes=(x32,))
        S.add(eng, CALL("tensor_tensor", x32[:, k, hs], x32[:, k, hs], rstd[:, :], ALU.mult), reads=(x32, rstd), writes=(x32,))
        S.add("act", CALL("activation", x32[:, k, hs], x32[:, k, hs], AF.Identity, bias=gb[:, 8 + k:9 + k], scale=gb[:, k:k + 1]),
              reads=(x32, gb), writes=(x32,))


def ffn_stage(S, c, xin, d_in, xout, d_out, wgu_l, wd_l, gb_ap, T):
    x32s, xbfs, act = c["f_x32"], c["f_xbf"], c["f_act"]
    wgu, wd = c["f_wgu"], c["f_wd"]
    ps = c["ps"]
    pg, pu, py = [ps[0], ps[1]], [ps[2], ps[3]], [ps[4], ps[0]]
    sg = c["f_sg"]
    gb = c["f_gb"]
    xin_v = xin.rearrange("(k p) t -> p k t", p=128)
    xout_v = xout.rearrange("(k p) t -> p k t", p=128)
    S.dma("sp", gb, gb[:, :], gb_ap)
    nst = T // 1024

    def load_x(st):
        x32, xbf = x32s[st % 2], xbfs[st % 2]
        S.dma("sp", x32, x32[:, :, :], xin_v[:, :, st * 1024:st * 1024 + 1024], reads=(d_in,))
        for k in range(8):
            S.dma("pool", xbf, xbf[:, k, :], xin_v[:, k, st * 1024:st * 1024 + 1024], reads=(d_in,))

    load_x(0)
    for st in range(nst):
        T0 = st * 1024
        x32, xbf = x32s[st % 2], xbfs[st % 2]
        if st + 1 < nst:
            load_x(st + 1)
        it = 0
        for blk in range(NBLK):
            wt = wgu[blk % 2]
            S.dma("sp", wt, wt[:, :, :], wgu_l[blk, :, :, :], reads=(c["d_wbf"],))
            for jj in range(2):
                j = blk * 2 + jj
                for half in range(2):
                    tg, tu = pg[it % 2], pu[it % 2]
                    tsg = sg[it % 2]
                    it += 1
                    hs = slice(half * 512, half * 512 + 512)
                    for k in range(8):
                        S.add("pe", CALL("matmul", tg[:, :], wt[:, k, jj * 128:(jj + 1) * 128], xbf[:, k, hs], start=(k == 0), stop=(k == 7)),
                              reads=(wt, xbf), writes=(tg,))
                    for k in range(8):
                        S.add("pe", CALL("matmul", tu[:, :], wt[:, k, 256 + jj * 128:256 + (jj + 1) * 128], xbf[:, k, hs], start=(k == 0), stop=(k == 7)),
                              reads=(wt, xbf), writes=(tu,))
                    S.add("act", CALL("activation", tsg[:, :], tg[:, :], AF.Silu), reads=(tg,), writes=(tsg,))
                    S.add("dve", CALL("tensor_tensor", act[:, j, hs], tsg[:, :], tu[:, :], ALU.mult), reads=(tsg, tu), writes=(act,))
        it = 0
        for cc in range(8):
            wt = wd[cc % 2]
            S.dma("sp", wt, wt[:, :, :], wd_l[cc, :, :, :], reads=(c["d_wbf"],))
            for half in range(2):
                ty = ps[4] if it % 2 == 0 else ps[1]
                it += 1
                hs = slice(half * 512, half * 512 + 512)
                for j in range(22):
                    S.add("pe", CALL("matmul", ty[:, :], wt[:, j, :], act[:, j, hs], start=(j == 0), stop=(j == 21)), reads=(wt, act), writes=(ty,))
                S.add("dve", CALL("scalar_tensor_tensor", x32[:, cc, hs], x32[:, cc, hs], 2.0 * ALPHA, ty[:, :], ALU.mult, ALU.add),
                      reads=(ty, x32), writes=(x32,))
        for half in range(2):
            hs = slice(half * 512, half * 512 + 512)
            ln_tile(S, c, x32, hs, gb, 0)
            S.dma("sp", d_out, xout_v[:, :, T0 + half * 512:T0 + half * 512 + 512], x32[:, :, hs], reads=(x32,))


def phase_O(S, c, catT, x1T, d_x1, xout, d_out, wout_l, gb_ap, T):
    ps = c["ps"]
    wo = c["o_w"]
    cat = c["o_cat"]
    x32s = c["o_x32"]
    gb = c["f_gb"]
    d_cat = c["d_cat"]
    catv = catT.rearrange("(k p) t -> p k t", p=128)
    x1v = x1T.rearrange("(k p) t -> p k t", p=128)
    xov = xout.rearrange("(k p) t -> p k t", p=128)
    S.dma("sp", gb, gb[:, :], gb_ap)
    S.dma("sp", wo, wo[:, :, :], wout_l[:, :, :], reads=(c["d_wbf"],))
    nt = T // 512

    def proj(t):
        ct, x32 = cat[t % 2], x32s[t % 3]
        ts = slice(t * 512, (t + 1) * 512)
        S.dma("sp", ct, ct[:, :, :], catv[:, :, ts], reads=(d_cat,))
        S.dma("sp", x32, x32[:, :, :], x1v[:, :, ts], reads=(d_x1,))
        for cc in range(8):
            ty = ps[cc % 4]
            for k in range(8):
                S.add("pe", CALL("matmul", ty[:, :], wo[:, k, cc * 128:(cc + 1) * 128], ct[:, k, :], start=(k == 0), stop=(k == 7)),
                      reads=(wo, ct), writes=(ty,))
            S.add("dve", CALL("scalar_tensor_tensor", x32[:, cc, :], x32[:, cc, :], ALPHA, ty[:, :], ALU.mult, ALU.add), reads=(ty, x32), writes=(x32,))

    proj(0)
    for t in range(nt):
        if t + 1 < nt:
            proj(t + 1)
        x32 = x32s[t % 3]
        ln_tile(S, c, x32, slice(0, 512), gb, 1)
        S.dma("sp", d_out, xov[:, :, t * 512:(t + 1) * 512], x32[:, :, :], reads=(x32,))


def alloc_all(S, c, T):
    NB = T // 128
    NC = T // 32
    c["ps"] = [S.ps(f"ps{i}", [128, 512]) for i in range(7)]
    c["psTb"] = S.ps("psTb", [128, 1024], BF16)
    c["psTf"] = Tile("psTf", c["psTb"].t[:, :].bitcast(F32))
    c["psTf"].psum = True
    for nm, shp, dt_ in (("ones32", [128, 128], F32), ("epsT", [128, 4], F32), ("tri", [128, 128], F32), ("identb", [128, 128], BF16),
                         ("gconst", [128, GCW], F32)):
        c[nm] = S.sb(nm, shp, dt_)
    S.make_arena(186 * 1024)
    for n in ("d_hT", "d_vtm", "d_sm", "d_cat", "d_gsm", "d_xa", "d_xb", "d_xc", "d_xin", "d_out"):
        c[n] = Tile(n, None)

    def A(name, shape, dt=F32):
        c[name] = S.av(name, shape, dt)

    def AL(name, n, shape, dt=F32):
        c[name] = [S.av(f"{name}{i}", shape, dt) for i in range(n)]
    for phase in ("F", "O"):
        S.arena_reset()
        A("f_gb", [128, 16]); AL("f_zsq", 2, [128, 512]); A("f_mean", [128, 512]); A("f_rstd", [128, 512]); A("f_msq", [128, 512])
        if phase == "F":
            AL("f_x32", 2, [128, 8, 1024]); AL("f_xbf", 2, [128, 8, 1024], BF16); A("f_act", [128, 22, 1024], BF16)
            AL("f_wgu", 2, [128, 8, 512], BF16); AL("f_wd", 2, [128, 22, 128], BF16); AL("f_sg", 2, [128, 512])
        else:
            A("o_w", [128, 8, 1024], BF16); AL("o_cat", 2, [128, 8, 512], BF16); AL("o_x32", 3, [128, 8, 512])
    S.arena_reset()
    A("b1_xbf", [128, 8, T], BF16); AL("b1_w", 2, [128, 8, 512], BF16); AL("b1_stg", 3, [128, 2048]); A("b1_wtm", [128, 8, 272], BF16)
    S.arena_reset()
    A("a_Kc", [128, 2, T], BF16); A("a_V", [128, T // 128, 128], BF16); A("a_Qall", [128, T // 512, 512], BF16); AL("a_Qp", 4, [128, 512], BF16)
    AL("a_ld", 4, [128, 512]); AL("a_cs", 2, [128, 2, 512]); AL("a_tmp", 2, [128, 512]); AL("a_P", 4, [128, 512], BF16)
    A("a_mask", [128, 4, 512], BF16); A("a_onesb", [128, 64], BF16); A("a_lam", [128, 140]); AL("a_sb64", 6, [64, 512])
    AL("a_aob", 2, [64, 512], BF16); A("a_subw", [64, 1]); AL("a_xo", 2, [128, 512]); A("a_sel", [128, 64])
    for phase in ("M", "G"):
        S.arena_reset()
        A("sm128", [128, NB, 16]); AL("m_xin", 2, [128, 515]); AL("m_acc", 2, [128, 512])
        if phase == "M":
            A("m_cwx", [64, 8, 4]); A("m_cwbc", [128, 4, 4]); A("m_cbx", [64, 8]); A("m_cbbc", [128, 4]); A("m_Dt", [64, 8]); A("m_nw", [64, 8])
            A("m_dt", [128, NB, 8]); A("m_a", [128, NB, 8]); A("m_acs", [128, NB, 8]); A("m_bc", [128, 16])
            A("m_xsT", [64, 8, 512]); A("m_xsb", [64, 8, 512], BF16); A("m_BT", [128, 2, 512], BF16); A("m_CT", [128, 2, 512], BF16)
            A("m_xs_tok", [128, 4, 512], BF16); A("m_B_tok", [128, 4, 256], BF16)
            A("m_rhsR", [128, 8, 128]); A("m_ER", [128, 8, 128]); A("m_E", [128, 8, 128]); A("m_Gm", [128, 2, 128])
            A("m_Mp", [128, 8, 128], BF16); A("m_Cdec", [128, 8, 128], BF16); A("m_xdt", [128, 8, 64], BF16); A("m_xsw", [128, 8, 64], BF16)
            A("m_wv", [128, 8]); A("m_H", [128, 8, 64]); A("m_Hbf", [128, 8, 64], BF16); A("m_ysb", [64, 8, 512])
            AL("m_mz", 2, [64, 512]); AL("m_ysq", 2, [64, 512]); A("m_rs", [64, 512]); AL("m_yo", 2, [64, 512], BF16)
        else:
            A("g_cw", [64, 12, 4]); A("g_nw", [64, 1]); A("g_bc", [128, 8])
            for k in ("g_g", "g_beta", "g_lnb", "g_gcs", "g_c1", "g_glast", "g_negb", "g_c2"):
                A(k, [128, NB, 4])
            A("g_stage", [128, NB, 8]); A("g_sm32", [32, NC, 8]); A("g_dcb", [128, NB, 4, 4]); A("g_rhs16", [128, NB, 4, 4])
            for k in ("g_qT", "g_kT", "g_vT"):
                A(k, [64, 4, 512])
            for k in ("g_qTb", "g_kTb", "g_vTb"):
                A(k, [64, 4, 512], BF16)
            AL("g_sq", 2, [64, 512]); A("g_rhsR", [128, 4, 128]); A("g_rhsR2", [128, 4, 128]); A("g_E1", [128, 4, 128]); A("g_E2", [128, 4, 128])
            AL("g_X", 2, [128, 4, 128]); AL("g_XT", 2, [128, 4, 128]); A("g_TT", [128, 4, 128]); A("g_Tb", [128, 4, 128], BF16)
            for k in ("g_ktok", "g_vtok", "g_kbg", "g_vb"):
                A(k, [128, 4, 64], BF16)
            A("g_wTb", [64, 4, 128], BF16); A("g_ERg", [64, 4, 128]); A("g_qd", [64, 4, 128], BF16)
            A("g_a3", [32, 4, 32]); A("g_attc", [32, 4, 32], BF16); A("g_kdec", [32, 4, 64], BF16); A("g_usb", [32, 4, 64]); A("g_vnb", [32, 4, 64], BF16)
            A("g_S", [64, 4, 64]); A("g_Sbf", [64, 4, 64], BF16); A("g_oT", [64, 4, 512]); A("g_gz4", [64, 4, 512]); A("g_rs", [64, 512]); AL("g_go", 2, [64, 512], BF16)


def pack_ffn(wgu, wd):
    wgu_l = np.empty((NBLK, 128, 8, 512), np.float32)
    wk = wgu.reshape(8, 128, 2 * FF)
    for blk in range(NBLK):
        wgu_l[blk, :, :, 0:256] = wk[:, :, blk * 256:(blk + 1) * 256].transpose(1, 0, 2)
        wgu_l[blk, :, :, 256:512] = wk[:, :, FF + blk * 256:FF + (blk + 1) * 256].transpose(1, 0, 2)
    wd_l = np.ascontiguousarray(wd.reshape(22, 128, 8, 128).transpose(2, 1, 0, 3))
    return wgu_l, wd_l


def pack_gb(g, b):
    return np.ascontiguousarray(np.concatenate([g.reshape(8, 128).T, b.reshape(8, 128).T], 1))


def pack_layer(P):
    out = {}
    out["f1_wgu"], out["f1_wd"] = pack_ffn(P["ffn1_w_gu"], P["ffn1_w_down"])
    out["f2_wgu"], out["f2_wd"] = pack_ffn(P["ffn2_w_gu"], P["ffn2_w_down"])
    out["gb1"] = pack_gb(P["ln1_g"], P["ln1_b"])
    out["gb2"] = pack_gb(P["ln2_g"], P["ln2_b"])
    out["gb3"] = pack_gb(P["ln3_g"], P["ln3_b"])
    out.update(pack_mixer(P))
    out.update(pack_mamba(P))
    out.update(pack_gdn(P))
    out["wout"] = np.ascontiguousarray(P["w_out"].reshape(8, 128, 1024).transpose(1, 0, 2))
    return out


LAYER_KEYS = ("f1_wgu", "f1_wd", "f2_wgu", "f2_wd", "gb1", "gb2", "gb3", "wfm", "wtm", "da_lambda", "subw",
              "m_cwx", "m_cwbc", "m_cbx", "m_cbbc", "m_dtb", "m_alog", "m_D", "m_nw", "g_cw", "g_dtb", "g_alog", "g_nw", "wout")


def pack_all(inputs, nlayers):
    layers = [pack_layer({k: np.asarray(v[l], np.float32) for k, v in inputs.items() if k != "x"}) for l in range(nlayers)]
    return {k: np.ascontiguousarray(np.stack([lay[k] for lay in layers])) for k in LAYER_KEYS}


def build_program(T, nlayers, shapes, lam_inits, first_layer=0):
    nc = bass.Bass("TRN2", target_bir_lowering=False)
    S = Sched(nc)
    aps = {}
    for k, (shp, is_bf) in shapes.items():
        aps[k] = nc.dram_tensor(k, list(shp), BF16 if is_bf else F32, kind="ExternalInput").ap()
    yT = nc.dram_tensor("yT", [D, T], F32, kind="ExternalOutput").ap()
    xa = nc.dram_tensor("xa", [D, T], F32).ap()
    xb = nc.dram_tensor("xb", [D, T], F32).ap()
    xc = nc.dram_tensor("xc", [D, T], F32).ap()
    hT = nc.dram_tensor("hT", [NROWS, T], F32).ap()
    v_tm = nc.dram_tensor("v_tm", [T, 256], F32).ap()
    sm_tm = nc.dram_tensor("sm_tm", [T, 16], F32).ap()
    gsm = nc.dram_tensor("gsm", [T, 8], F32).ap()
    catT = nc.dram_tensor("catT", [D, T], BF16).ap()
    c = {}
    alloc_all(S, c, T)
    S.add("dve", CALL("memset", c["ones32"][:, :], 1.0), writes=(c["ones32"],))
    S.add("dve", CALL("memset", c["epsT"][:, 0:1], 4 * EPS), writes=(c["epsT"],))
    S.add("dve", CALL("memset", c["epsT"][:, 1:2], EPS), writes=(c["epsT"],))
    S.add("dve", CALL("memset", c["epsT"][:, 2:3], 1e-6), writes=(c["epsT"],))
    gc = c["gconst"]
    S.dma("sp", gc, gc[:, :], aps["gconst"])
    S.add("dve", CALL("tensor_copy", c["tri"][:, :], gc[:, 0:128]), reads=(gc,), writes=(c["tri"],))
    S.add("dve", CALL("tensor_copy", c["identb"][:, :], gc[:, 384:512]), reads=(gc,), writes=(c["identb"],))
    kstop = ""
    BIGW = ("f1_wgu", "f1_wd", "f2_wgu", "f2_wd", "wfm", "wtm", "wout")
    wbf = {}
    d_wbf = Tile("d_wbf", None)
    for kname in BIGW:
        wbf[kname] = nc.dram_tensor(kname + "_bf", list(shapes[kname][0]), BF16).ap()

    def convert_layer(li):
        for kname in BIGW:
            per = 1
            for d_ in shapes[kname][0][1:]:
                per *= d_
            rows = per // 2048
            src = aps[kname][li].flatten().rearrange("(r e) -> r e", e=2048)
            dst = wbf[kname][li].flatten().rearrange("(r e) -> r e", e=2048)
            step = 704 if rows > 704 else rows
            for r0 in range(0, rows, step):
                r1 = min(rows, r0 + step)
                S.dma("pool", d_wbf, dst[r0:r1, :], src[r0:r1, :])

    convert_layer(0)
    c["d_wbf"] = d_wbf
    cur, d_cur = aps["xT"], c["d_xin"]
    for li in range(nlayers):
        W = {k: (wbf[k][li] if k in wbf else aps[k][li]) for k in LAYER_KEYS}
        last = li == nlayers - 1
        S.barrier()
        ffn_stage(S, c, cur, d_cur, xb, c["d_xb"], W["f1_wgu"], W["f1_wd"], W["gb1"], T)
        if kstop == "F1":
            break
        S.barrier()
        phase_B1(S, c, xb, hT, v_tm, sm_tm, W["wfm"], W["wtm"], T)
        if kstop == "B1":
            break
        S.barrier()
        if li + 1 < nlayers:
            convert_layer(li + 1)
        phase_A(S, c, hT, v_tm, catT, aps["CT"], aps["ST"], aps["amask"], W["da_lambda"], W["subw"], lam_inits[li], T)
        if kstop == "A":
            break
        S.barrier()
        phase_M(S, c, hT, sm_tm, catT, W, T)
        if kstop == "M":
            break
        S.barrier()
        phase_G(S, c, hT, sm_tm, catT, W, gsm, T)
        if kstop == "G":
            break
        S.barrier()
        phase_O(S, c, catT, xb, c["d_xb"], xc, c["d_xc"], W["wout"], W["gb2"], T)
        if kstop == "O":
            break
        S.barrier()
        dst, d_dst = (yT, c["d_out"]) if last else (xa, c["d_xa"])
        ffn_stage(S, c, xc, c["d_xc"], dst, d_dst, W["f2_wgu"], W["f2_wd"], W["gb3"], T)
        cur, d_cur = xa, c["d_xa"]
    S.barrier()
    S.add("sp", None, reads=(c["d_out"],))
    st = S.emit()
    return nc, st


def lambda_init(l):
    return 0.8 - 0.6 * math.exp(-0.3 * l)


SEQ = 4096
NCORES = 4


def kernel(**inputs):
    inputs = {k: np.asarray(v) for k, v in inputs.items()}
    x = inputs["x"].astype(np.float32, copy=False)
    packed = pack_all(inputs, DEPTH)
    CT, ST = rope_consts(SEQ)
    common = {"gconst": g_consts(), "CT": CT, "ST": ST, "amask": attn_mask()}
    in_maps = []
    for b in range(NCORES):
        m = dict(packed)
        m.update(common)
        m["xT"] = np.ascontiguousarray(x[b].T)
        in_maps.append(m)
    shapes = {k: (v.shape, v.dtype == ml_dtypes.bfloat16) for k, v in in_maps[0].items()}
    nc, _ = build_program(SEQ, DEPTH, shapes, [lambda_init(l) for l in range(DEPTH)])
    res = run_bass_kernel_spmd(nc, in_maps, core_ids=list(range(NCORES)))
    out = np.stack([np.asarray(res.results[b]["yT"]).T for b in range(NCORES)])
    return np.ascontiguousarray(out.astype(np.float32, copy=False))
```
